# Optimizing a Trainium2 kernel written in Bass

```python
import jax, jax.numpy as jnp
from jax import lax
import numpy as np

D_MODEL = 2048
BATCH = 16
SEQ = 2048
DEPTH = 2
DEC_BATCH = 32
DEC_SEQ = 64
PAST_LEN = 4096

CHUNK = 64
N_META = 16
EPS = 1e-6
GLA_HEADS = 4
GLA_DK = D_MODEL // 16
GLA_DV = D_MODEL // 8
GLA_RANK = 16
GLA_TAU = 16.0
GLA_BLOCK = CHUNK
SB_HEAD_DIM = 128
SB_HEADS = D_MODEL // 256
SB_Q_BLOCK = 128

GLA_QK = GLA_HEADS * GLA_DK
GLA_V = GLA_HEADS * GLA_DV
SB_W = SB_HEADS * SB_HEAD_DIM
IN_SIZES = (GLA_QK, GLA_QK, GLA_V, GLA_V, GLA_RANK, SB_W, SB_W, SB_W, SB_W, D_MODEL, D_MODEL)
IN_SPLITS = tuple(int(s) for s in np.cumsum(IN_SIZES)[:-1])
IN_COLS = int(sum(IN_SIZES))

kernel_name = "gla_stickbreak_hybrid_stream_step"


def rms_norm(x, gain):
    xf = x.astype(jnp.float32)
    y = xf * lax.rsqrt(jnp.mean(xf * xf, axis=-1, keepdims=True) + EPS)
    return (y * gain.astype(jnp.float32)).astype(x.dtype)


def gla_recurrence(q, k, v, log_a, s0):
    B, T, H, DK = q.shape
    DV = v.shape[-1]
    C = GLA_BLOCK
    n = -(-T // C)
    pad = n * C - T

    def blocks(a):
        a = jnp.pad(a.astype(jnp.float32), ((0, 0), (0, pad), (0, 0), (0, 0)))
        return a.reshape(B, n, C, H, a.shape[-1]).transpose(1, 0, 3, 2, 4)

    qc, kc, vc, ac = blocks(q * DK ** -0.5), blocks(k), blocks(v), blocks(log_a)
    causal = jnp.tril(jnp.ones((C, C), dtype=bool))

    def step(S, blk):
        qi, ki, vi, ai = blk
        b = jnp.cumsum(ai, axis=2)
        diff = b[:, :, :, None, :] - b[:, :, None, :, :]
        decay = jnp.exp(jnp.where(causal[:, :, None], diff, -jnp.inf))
        scores = jnp.einsum('bhtk,bhsk,bhtsk->bhts', qi, ki, decay)
        o = (jnp.einsum('bhtk,bhkv->bhtv', qi * jnp.exp(b), S)
             + jnp.einsum('bhts,bhsv->bhtv', scores, vi))
        b_end = b[:, :, -1, :]
        S = (jnp.exp(b_end)[..., None] * S
             + jnp.einsum('bhsk,bhsv->bhkv', ki * jnp.exp(b_end[:, :, None, :] - b), vi))
        return S, o

    S, o = lax.scan(step, s0.astype(jnp.float32), (qc, kc, vc, ac))
    o = o.transpose(1, 0, 3, 2, 4).reshape(B, n * C, H, DV)[:, :T]
    return o, S


def stick_breaking_attention(q, k, v, q_offset):
    B, Tq, H, D = q.shape
    Tk = k.shape[1]
    qf = q.astype(jnp.float32) * D ** -0.5
    kf = k.astype(jnp.float32)
    vf = v.astype(jnp.float32)
    outs = []
    for start in range(0, Tq, SB_Q_BLOCK):
        stop = min(start + SB_Q_BLOCK, Tq)
        n_keys = max(1, min(q_offset + stop - 1, Tk))
        z = jnp.einsum('bqhd,bkhd->bhqk', qf[:, start:stop], kf[:, :n_keys])
        q_pos = q_offset + jnp.arange(start, stop)
        mask = jnp.arange(n_keys)[None, :] < q_pos[:, None]
        log_beta = jax.nn.log_sigmoid(z)
        log_om = jnp.where(mask, jax.nn.log_sigmoid(-z), 0.0)
        tail = lax.cumsum(log_om, axis=3, reverse=True) - log_om
        w = jnp.where(mask, jnp.exp(log_beta + tail), 0.0)
        outs.append(jnp.einsum('bhqk,bkhd->bqhd', w, vf[:, :n_keys]))
    return jnp.concatenate(outs, axis=1).astype(v.dtype)


def mixer_layer(x, k_past, v_past, s0, pre_gain, w_in, w_a2, b_a, gla_gain, w_up_gla, w_up_sb, w_o, post_gain):
    B, T, _ = x.shape
    h = rms_norm(x, pre_gain)
    (g_q, g_k, g_v, g_z, g_r, s_q, s_k, s_v, s_z, m_gla, m_sb) = jnp.split(h @ w_in, IN_SPLITS, axis=-1)

    log_a = jax.nn.log_sigmoid((g_r @ w_a2 + b_a).astype(jnp.float32)) / GLA_TAU
    o_gla, s_new = gla_recurrence(g_q.reshape(B, T, GLA_HEADS, GLA_DK),
                                  g_k.reshape(B, T, GLA_HEADS, GLA_DK),
                                  g_v.reshape(B, T, GLA_HEADS, GLA_DV),
                                  log_a.reshape(B, T, GLA_HEADS, GLA_DK), s0)
    o_gla = rms_norm(o_gla, gla_gain).astype(x.dtype).reshape(B, T, GLA_V)
    y_gla = (o_gla * jax.nn.silu(g_z)) @ w_up_gla

    k_new = s_k.reshape(B, T, SB_HEADS, SB_HEAD_DIM)
    v_new = s_v.reshape(B, T, SB_HEADS, SB_HEAD_DIM)
    if k_past is None:
        k_all, v_all, offset = k_new, v_new, 0
    else:
        k_all = jnp.concatenate([k_past.astype(k_new.dtype), k_new], axis=1)
        v_all = jnp.concatenate([v_past.astype(v_new.dtype), v_new], axis=1)
        offset = k_past.shape[1]
    o_sb = stick_breaking_attention(s_q.reshape(B, T, SB_HEADS, SB_HEAD_DIM), k_all, v_all, offset)
    y_sb = (o_sb.reshape(B, T, SB_W) * jax.nn.silu(s_z)) @ w_up_sb

    merged = jax.nn.sigmoid(m_gla) * y_gla + jax.nn.sigmoid(m_sb) * y_sb
    y = x + rms_norm(merged @ w_o, post_gain)
    return y, k_new, v_new, s_new.astype(x.dtype)


def setup_inputs(seed: int = 0) -> dict:
    key = jax.random.key(seed)
    ks = jax.random.split(key, 16)
    f32 = jnp.float32
    nrm = lambda k, shape, scale: jax.random.normal(k, shape, f32) * scale
    return {
        "x_prompt": nrm(ks[0], (BATCH, SEQ, D_MODEL), 1.0),
        "x_sample": nrm(ks[1], (DEC_BATCH, DEC_SEQ, D_MODEL), 1.0),
        "cache_sb_k": nrm(ks[2], (DEPTH, DEC_BATCH, PAST_LEN, SB_HEADS, SB_HEAD_DIM), 1.0),
        "cache_sb_v": nrm(ks[3], (DEPTH, DEC_BATCH, PAST_LEN, SB_HEADS, SB_HEAD_DIM), 1.0),
        "state_gla": nrm(ks[4], (DEPTH, DEC_BATCH, GLA_HEADS, GLA_DK, GLA_DV), 1.0),
        "meta_tokens": nrm(ks[5], (N_META, D_MODEL), 1.0),
        "pre_gain": 1.0 + nrm(ks[6], (DEPTH, D_MODEL), 0.01),
        "w_in": nrm(ks[7], (DEPTH, D_MODEL, IN_COLS), D_MODEL ** -0.5),
        "w_a2": nrm(ks[8], (DEPTH, GLA_RANK, GLA_QK), GLA_RANK ** -0.5),
        "b_a": nrm(ks[9], (DEPTH, GLA_QK), 0.01),
        "gla_gain": 1.0 + nrm(ks[10], (DEPTH, GLA_HEADS, GLA_DV), 0.01),
        "w_up_gla": nrm(ks[11], (DEPTH, GLA_V, D_MODEL), GLA_V ** -0.5),
        "w_up_sb": nrm(ks[12], (DEPTH, SB_W, D_MODEL), SB_W ** -0.5),
        "w_o": nrm(ks[13], (DEPTH, D_MODEL, D_MODEL), D_MODEL ** -0.5),
        "post_gain": 1.0 + nrm(ks[14], (DEPTH, D_MODEL), 0.01),
    }


def reference(x_prompt, x_sample, cache_sb_k, cache_sb_v, state_gla, meta_tokens, pre_gain, w_in, w_a2,
              b_a, gla_gain, w_up_gla, w_up_sb, w_o, post_gain):
    B = x_prompt.shape[0]
    meta = jnp.broadcast_to(meta_tokens.astype(x_prompt.dtype)[None], (B, N_META, D_MODEL))
    hp = jnp.concatenate([meta, x_prompt], axis=1)
    hs = x_sample
    zero_state = jnp.zeros((B, GLA_HEADS, GLA_DK, GLA_DV), jnp.float32)
    kp, vp, sp, ks_, vs_, ss_ = [], [], [], [], [], []
    for l in range(DEPTH):
        hp, k_, v_, s_ = mixer_layer(hp, None, None, zero_state, pre_gain[l], w_in[l], w_a2[l], b_a[l],
                                     gla_gain[l], w_up_gla[l], w_up_sb[l], w_o[l], post_gain[l])
        kp.append(k_); vp.append(v_); sp.append(s_)
        hs, k_, v_, s_ = mixer_layer(hs, cache_sb_k[l], cache_sb_v[l], state_gla[l], pre_gain[l], w_in[l],
                                     w_a2[l], b_a[l], gla_gain[l], w_up_gla[l], w_up_sb[l], w_o[l],
                                     post_gain[l])
        ks_.append(k_); vs_.append(v_); ss_.append(s_)
    return (hp[:, N_META:], hs, jnp.stack(kp), jnp.stack(vp), jnp.stack(sp),
            jnp.stack(ks_), jnp.stack(vs_), jnp.stack(ss_))
```

```python
from contextlib import ExitStack
import numpy as np
import concourse.bass as bass
import concourse.mybir as mybir
from concourse.bass_utils import run_bass_kernel_spmd

F32 = mybir.dt.float32
BF16 = mybir.dt.bfloat16
AF = mybir.ActivationFunctionType
ALU = mybir.AluOpType

D = 2048
KC = 16
N_META = 16
EPS = 1e-6
IN_COLS = 11280
SEC = dict(gq=(0, 512), gk=(512, 512), gv=(1024, 1024), gz=(2048, 1024), gr=(3072, 16),
           sq=(3088, 1024), sk=(4112, 1024), sv=(5136, 1024), sz=(6160, 1024),
           mg=(7184, 2048), ms=(9232, 2048))
G = 256
NW = 4
ENGS = ("pe", "act", "dve", "pool", "sp")
N_DMA_SEMS = 8
STQ = "sp"


class Op:
    __slots__ = ("eng", "fn", "deps", "is_dma", "sem", "val", "signal", "idx")

    def __init__(self, eng, fn, is_dma):
        self.eng = eng
        self.fn = fn
        self.deps = []
        self.is_dma = is_dma
        self.sem = None
        self.val = None
        self.signal = False
        self.idx = None


class Sched:
    def __init__(self):
        self.ops = []
        self.last_w = {}
        self.readers = {}
        self.fence = None
        self.last_eng = {}
        self.dmas_since = []
        self.muted = False
        self.maxops = None

    def op(self, eng, fn, reads=(), writes=(), dma=False, nobar=False, nofence=False):
        if self.muted or (self.maxops is not None and len(self.ops) >= self.maxops):
            return None
        o = Op(eng, fn, dma)
        o.idx = len(self.ops)
        deps = set()
        for k in reads:
            w = self.last_w.get(k)
            if w is not None:
                deps.add(w)
        for k in writes:
            w = self.last_w.get(k)
            if w is not None:
                deps.add(w)
            for r in self.readers.get(k, ()):
                deps.add(r)
        for k in writes:
            self.last_w[k] = o
            self.readers[k] = []
        for k in reads:
            self.readers.setdefault(k, []).append(o)
        if self.fence is not None and not nofence:
            deps.add(self.fence)
        deps.discard(o)
        o.deps = sorted(deps, key=lambda d: d.idx)
        self.ops.append(o)
        if dma:
            if not nobar:
                self.dmas_since.append(o)
        else:
            self.last_eng[eng] = o
        return o

    def barrier(self, fn):
        if self.muted:
            return None
        o = Op("pool", fn, False)
        o.idx = len(self.ops)
        deps = set(self.last_eng.values()) | set(self.dmas_since)
        if self.fence is not None:
            deps.add(self.fence)
        o.deps = sorted(deps, key=lambda d: d.idx)
        self.ops.append(o)
        self.fence = o
        self.dmas_since = []
        self.last_eng = {"pool": o}
        self.last_w = {k: v for k, v in self.last_w.items()
                       if isinstance(k, tuple) and (str(k[0]).startswith('wb_') or k[0] == 'W')}
        self.readers = {k: v for k, v in self.readers.items() if isinstance(k, tuple) and k[0] == 'W'}
        return o

    def emit(self, nc, stack):
        engines = {"pe": "tensor", "act": "scalar", "dve": "vector", "pool": "gpsimd", "sp": "sync"}
        csem = {e: stack.enter_context(nc.semaphore("cs_" + e)) for e in ENGS}
        dsem = {e: [stack.enter_context(nc.semaphore("ds_%s_%d" % (e, i))) for i in range(N_DMA_SEMS)]
                for e in ("act", "pool", "sp")}
        pos = {}
        cnt = {e: 0 for e in ENGS}
        for o in self.ops:
            pos[o] = cnt[o.eng]
            cnt[o.eng] += 1

        def need_sync(o, d):
            if d.is_dma or d.eng != o.eng:
                return True
            return o.eng != "pe" and (pos[o] - pos[d]) <= 4

        for o in self.ops:
            for d in o.deps:
                if need_sync(o, d):
                    d.signal = True
        ccount = {e: 0 for e in ENGS}
        dcount = {e: [0] * N_DMA_SEMS for e in dsem}
        dnext = {e: 0 for e in dsem}
        dprev = {}
        for o in self.ops:
            if o.is_dma:
                e = o.eng
                i = dnext[e]
                dnext[e] = (i + 1) % N_DMA_SEMS
                dcount[e][i] += 16
                o.sem = dsem[e][i]
                o.val = dcount[e][i]
                p = dprev.get((e, i))
                if p is not None and p not in o.deps:
                    o.deps.append(p)
                dprev[(e, i)] = o
            elif o.signal:
                ccount[o.eng] += 1
                o.sem = csem[o.eng]
                o.val = ccount[o.eng]
        per_eng = {e: [o for o in self.ops if o.eng == e] for e in ENGS}
        block = stack.enter_context(nc.Block())

        def make(e):
            def body(eng):
                waited = {}
                for o in per_eng[e]:
                    for d in o.deps:
                        if not need_sync(o, d):
                            continue
                        key = d.sem.num
                        if waited.get(key, 0) >= d.val:
                            continue
                        eng.wait_ge(d.sem, d.val)
                        waited[key] = d.val
                    ins = o.fn(eng)
                    if o.is_dma:
                        ins.then_inc(o.sem, 16)
                    elif o.signal:
                        ins.then_inc(o.sem, 1)
                if e in dsem:
                    for i, s in enumerate(dsem[e]):
                        if dcount[e][i] > 0 and waited.get(s.num, 0) < dcount[e][i]:
                            eng.wait_ge(s, dcount[e][i])
            return body

        for e in ENGS:
            if per_eng[e]:
                getattr(block, engines[e])(make(e))


class Arena:
    def __init__(self, nc, st, nbytes):
        self.cap = nbytes
        self.t = st.enter_context(nc.sbuf_tensor("arena", [128, nbytes // 2], BF16))
        self.top = 0

    def alloc(self, shape, dt):
        n = 1
        for s in shape:
            n *= s
        isz = 4 if dt == F32 else 2
        off = (self.top + 3) // 4 * 4
        self.top = off + n * isz
        assert self.top <= self.cap, "arena overflow %d > %d" % (self.top, self.cap)
        ap = self.t[:, off // 2: off // 2 + n * isz // 2]
        if dt == F32:
            ap = ap.bitcast(F32)
        if len(shape) == 2:
            ap = ap.rearrange("p (a b) -> p a b", b=shape[1])
        elif len(shape) == 3:
            ap = ap.rearrange("p (a b c) -> p a b c", b=shape[1], c=shape[2])
        return ap


class Rot:
    def __init__(self, items):
        self.items = items
        self.i = 0

    def get(self):
        it = self.items[self.i]
        self.i = (self.i + 1) % len(self.items)
        return it


def build_program(cfg):
    NP, SEQ, NS, PAST, DEPTH = cfg["NP"], cfg["SEQ"], cfg["NS"], cfg["PAST"], cfg["DEPTH"]
    T = N_META + SEQ
    assert SEQ % 512 == 0 and PAST % 128 == 0
    NT = SEQ // 512
    NKB = 1 + 4 * NT
    NPB = PAST // 128

    nc = bass.Bass("TRN2", target_bir_lowering=False)

    def din(name, shape):
        return nc.dram_tensor(name, list(shape), F32, kind="ExternalInput").ap()

    def dout(name, shape):
        return nc.dram_tensor(name, list(shape), F32, kind="ExternalOutput").ap()

    xp = din("xp", (NP, SEQ, D))
    xs = din("xs", (NS, 64, D))
    ck = din("ck", (DEPTH, NS, PAST, 1024))
    cv = din("cv", (DEPTH, NS, PAST, 1024))
    sg = din("sg", (DEPTH, NS, 4, 128, 256))
    meta = din("meta", (N_META, D))
    pre_g = din("pre_g", (DEPTH, 128, 16))
    post_g = din("post_g", (DEPTH, 128, D))
    w_in = din("w_in", (DEPTH, D, IN_COLS))
    w_a2 = din("w_a2", (DEPTH, 16, 512))
    b_a = din("b_a", (DEPTH, 128, 4))
    gla_g = din("gla_g", (DEPTH, 128, 8))
    w_ug = din("w_ug", (DEPTH, 1024, D))
    w_us = din("w_us", (DEPTH, 1024, D))
    w_o = din("w_o", (DEPTH, D, D))
    c_ident = din("c_ident", (128, 128))
    c_negtri = din("c_negtri", (128, 128))
    c_mask = din("c_mask", (128, 896))
    c_rst = din("c_rst", (2, 128, 512))

    yp = dout("yp", (NP, SEQ, D))
    ys = dout("ys", (NS, 64, D))
    kp = dout("kp", (DEPTH, NP, T, 1024))
    vp = dout("vp", (DEPTH, NP, T, 1024))
    gp = dout("gp", (DEPTH, NP, 4, 128, 256))
    ksn = dout("ksn", (DEPTH, NS, 64, 1024))
    vsn = dout("vsn", (DEPTH, NS, 64, 1024))
    gs = dout("gs", (DEPTH, NS, 4, 128, 256))

    wb_in = (nc.dram_tensor("wb_in", [DEPTH, D, IN_COLS], BF16, kind="ExternalOutput").ap() if cfg.get("dbg") else nc.dram_tensor("wb_in", [DEPTH, D, IN_COLS], BF16).ap())
    if cfg.get("dbg"):
        dbg_hT = nc.dram_tensor("dbg_hT", [128, KC * 512], F32, kind="ExternalOutput").ap()
        dbg_x = nc.dram_tensor("dbg_x", [128, D], F32, kind="ExternalOutput").ap()
        dbg_hb = nc.dram_tensor("dbg_hb", [128, D], F32, kind="ExternalOutput").ap()
        dbg_sm = nc.dram_tensor("dbg_sm", [128, 64], F32, kind="ExternalOutput").ap()
    wb_ug = nc.dram_tensor("wb_ug", [DEPTH, 1024, D], BF16).ap()
    wb_us = nc.dram_tensor("wb_us", [DEPTH, 1024, D], BF16).ap()
    wb_o = nc.dram_tensor("wb_o", [DEPTH, D, D], BF16).ap()
    yscr_p = nc.dram_tensor("yscr_p", [NP, T, D], F32).ap()
    yscr_s = nc.dram_tensor("yscr_s", [NS, 64, D], F32).ap()
    smeta = nc.dram_tensor("smeta", [4, 128, 256], F32).ap()

    S = Sched()
    S.maxops = cfg.get('maxops')
    st = ExitStack()
    with st:
        A = Arena(nc, st, 206 * 1024)
        psb = [st.enter_context(nc.psum_tensor("ps%d" % i, [128, 512], F32)) for i in range(8)]
        PS = [(psb[i][:], ("ps", i)) for i in range(8)]
        PS16 = [(psb[i][:].bitcast(BF16), ("ps", i)) for i in range(8)]
        rot_lo = Rot(list(range(4)))
        rot_all = Rot(list(range(8)))

        ident16 = A.alloc((128,), BF16)
        negtri16 = A.alloc((128,), BF16)
        negones16 = A.alloc((128,), BF16)
        onesmean16 = A.alloc((128,), BF16)
        mask16 = A.alloc((896,), BF16)
        mask32 = A.alloc((896,), F32)
        masks16 = A.alloc((8, 64), BF16)
        masks32 = A.alloc((8, 64), F32)
        rst = A.alloc((2, 512), F32)
        pg_fm = A.alloc((16,), F32)
        negba = A.alloc((4,), F32)
        gg = A.alloc((8,), F32)
        wa2_16 = A.alloc((512,), BF16)
        pgb = A.alloc((D,), F32)
        A.top = (A.top + 3) // 4 * 4
        u_off = A.top
        hT = A.alloc((KC, 512), BF16)
        ogT = A.alloc((8, 512), BF16)
        osbT = A.alloc((8, 512), BF16)
        u32_alias = A.t[:, u_off // 2: u_off // 2 + 4 * D * 2].bitcast(F32).rearrange("p (a b) -> p a b", b=D)
        Wb = [A.alloc((KC, G), BF16) for _ in range(NW)]
        wrot = Rot([(Wb[i], ("W", i)) for i in range(NW)])
        A.top = (A.top + 3) // 4 * 4
        kv_off = A.top
        KT = A.alloc((8, T), BF16)
        Vc = A.alloc((NKB, 1024), BF16)
        A.top = max(A.top, kv_off + 64 * 1024)
        kv_end = A.top
        S32p = A.alloc((4, 256), F32)
        S16p = A.alloc((4, 256), BF16)
        dummy = A.alloc((8,), F32)
        small = A.alloc((64,), F32)
        small_rot = Rot([(small[:, 4 * i: 4 * i + 4], ("small", i)) for i in range(16)])
        persist_top = A.top

        phase_ctr = [0]

        class Arena2:
            top = kv_off

            @staticmethod
            def alloc(shape, dt):
                save = A.top
                A.top = Arena2.top
                ap = A.alloc(shape, dt)
                Arena2.top = A.top
                assert Arena2.top <= kv_end, "arena2 overflow"
                A.top = save
                return ap

        def phase_reset():
            Arena2.top = kv_off
            phase_ctr[0] += 1
            if cfg.get('verbose'):
                print('phase', phase_ctr[0], 'ops', len(S.ops), 'arena top', A.top, 'persist', persist_top)
            if cfg.get('stop') is not None and phase_ctr[0] > cfg['stop']:
                S.muted = True
            S.barrier(lambda e: e.memset(dummy[:, 0:1], 0.0))
            A.top = persist_top

        S.op("pool", lambda e: e.dma_start(out=ident16, in_=c_ident), writes=["ident"], dma=True)
        S.op("pool", lambda e: e.dma_start(out=negtri16, in_=c_negtri), writes=["negtri"], dma=True)
        S.op("pool", lambda e: e.dma_start(out=mask16, in_=c_mask), writes=["mask16"], dma=True)
        S.op("sp", lambda e: e.dma_start(out=mask32, in_=c_mask), writes=["mask32"], dma=True)
        S.op("sp", lambda e: e.dma_start(out=rst, in_=c_rst.rearrange("a p n -> p a n")), writes=["rst"], dma=True)
        S.op("pool", lambda e: e.memset(negones16, -1.0), writes=["negones"])
        S.op("pool", lambda e: e.memset(onesmean16, 1.0 / 256.0), writes=["onesmean"])
        for h in range(8):
            S.op("pool", lambda e, h=h: e.tensor_copy(masks16[:, h, :], mask16[:, 384:448]),
                 reads=["mask16"], writes=["masks16"])
            S.op("pool", lambda e, h=h: e.tensor_copy(masks32[:, h, :], mask32[:, 384:448]),
                 reads=["mask32"], writes=["masks32"])

        conv_q = []

        def convert_layer(l):
            for c0 in range(0, IN_COLS, 512):
                c1 = min(IN_COLS, c0 + 512)
                conv_q.append((lambda e, c0=c0, c1=c1, l=l: e.dma_start(out=wb_in[l, :, c0:c1], in_=w_in[l, :, c0:c1]),
                               ("wb_in", l, c0 // 512)))
            for (dst, src, nm) in ((wb_ug, w_ug, "wb_ug"), (wb_us, w_us, "wb_us"), (wb_o, w_o, "wb_o")):
                for c0 in range(0, D, 512):
                    conv_q.append((lambda e, c0=c0, dst=dst, src=src, l=l: e.dma_start(out=dst[l, :, c0:c0 + 512],
                                                                                      in_=src[l, :, c0:c0 + 512]),
                                   (nm, l, c0 // 512)))

        def conv_some(k):
            for _ in range(min(k, len(conv_q))):
                fn, key = conv_q.pop(0)
                S.op("pool", fn, writes=[key], dma=True, nobar=True)

        def load_w(l, which, c0, ncols, kchunks):
            buf, key = wrot.get()
            if which == "in":
                src = wb_in[l, :, c0:c0 + ncols]
                rk = [("wb_in", l, c0 // 512), ("wb_in", l, (c0 + ncols - 1) // 512)]
            else:
                src = {"ug": wb_ug, "us": wb_us, "o": wb_o}[which][l, :, c0:c0 + ncols]
                rk = [("wb_" + which, l, c0 // 512)]
            src = src.rearrange("(kc p) g -> p kc g", p=128)
            dst = buf[:, 0:kchunks, 0:ncols]
            S.op("sp", lambda e: e.dma_start(out=dst, in_=src), reads=rk, writes=[key], dma=True, nofence=True)
            return dst, key

        def mm_group(ps_ap, pskey, pairs, reads):
            def fn(e):
                ins = None
                n = len(pairs)
                for i, (l_, r_) in enumerate(pairs):
                    ins = e.matmul(ps_ap, l_, r_, start=(i == 0), stop=(i == n - 1))
                return ins
            S.op("pe", fn, reads=reads, writes=[pskey])

        def proj_fm(l, sec, n, handler, blocks=None):
            c0, nc_ = SEC[sec]
            nblk = (nc_ + 127) // 128
            wcur = None
            for j in (blocks if blocks is not None else range(nblk)):
                gidx = (j * 128) // G
                if wcur is None or wcur[0] != gidx:
                    gc0 = c0 + gidx * G
                    gn = min(G, c0 + nc_ - gc0)
                    w_ap, w_key = load_w(l, "in", gc0, gn, KC)
                    wcur = (gidx, w_ap, w_key)
                _, w_ap, w_key = wcur
                m = min(128, nc_ - j * 128)
                o0 = j * 128 - gidx * G
                pi = rot_lo.get()
                ps_ap, ps_key = PS[pi]
                mm_group(ps_ap[0:m, 0:n], ps_key,
                         [(w_ap[:, kc, o0:o0 + m], hT[:, kc, 0:n]) for kc in range(KC)],
                         reads=[w_key, "hT"])
                handler(j, m, ps_ap[0:m, 0:n], ps_key)

        def proj_tm(l, sec, blocks, handler):
            c0, nc_ = SEC[sec]
            for g in range(nc_ // G):
                w_ap, w_key = load_w(l, "in", c0 + g * G, G, KC)
                for b, (off, sz) in enumerate(blocks):
                    pi = rot_lo.get()
                    ps_ap, ps_key = PS[pi]
                    mm_group(ps_ap[0:sz, 0:G], ps_key,
                             [(hT[:, kc, off:off + sz], w_ap[:, kc, 0:G]) for kc in range(KC)],
                             reads=[w_key, "hT"])
                    handler(b, off, sz, g, ps_ap[0:sz, 0:G], ps_key)

        def rstd_from_ssq(ssq_ap, key, sz, scale):
            if cfg.get('dbg_norstd'):
                return
            S.op("act", lambda e: e.activation(ssq_ap, ssq_ap, AF.Ln, bias=EPS, scale=scale), reads=[key], writes=[key])
            S.op("act", lambda e: e.activation(ssq_ap, ssq_ap, AF.Exp, scale=-0.5), reads=[key], writes=[key])

        def run_tile(l, tl):
            kind = tl["kind"]
            n = tl["n"]
            blocks = tl["blocks"]
            nb = len(blocks)
            last = (l == DEPTH - 1)

            phase_reset()
            xin = [A.alloc((D,), F32) for _ in range(2)]
            hbf = [A.alloc((D,), BF16) for _ in range(2)]
            for b, (off, sz) in enumerate(blocks):
                xi, xk = xin[b % 2], ("xin", b % 2)
                hb, hk = hbf[b % 2], ("hbf", b % 2)
                sm, smk = small_rot.get()
                src = tl["xsrc"][b]
                S.op("sp", lambda e, xi=xi, sz=sz, src=src: e.dma_start(out=xi[0:sz, :], in_=src),
                     reads=tl["xsrc_keys"][b], writes=[xk], dma=True)
                S.op("act", lambda e, xi=xi, hb=hb, sm=sm, sz=sz: e.activation(hb[0:sz, :], xi[0:sz, :], AF.Square,
                                                                                accum_out=sm[0:sz, 0:1]),
                     reads=[xk], writes=[hk, smk])
                rstd_from_ssq(sm[0:sz, 0:1], smk, sz, 1.0 / D)
                S.op("dve", lambda e, xi=xi, hb=hb, sm=sm, sz=sz: e.tensor_scalar(hb[0:sz, :], xi[0:sz, :], sm[0:sz, 0:1],
                                                                                   None, ALU.mult),
                     reads=[xk, smk], writes=[hk])
                for half in range(2):
                    pi = rot_lo.get()
                    p16, pk = PS16[pi]

                    def tfn(e, hb=hb, sz=sz, half=half, p16=p16):
                        ins = None
                        for q in range(8):
                            kc = half * 8 + q
                            ins = e.transpose(p16[:, q * 128:q * 128 + sz], hb[0:sz, kc * 128:(kc + 1) * 128],
                                              ident16[0:sz, 0:sz])
                        return ins
                    S.op("pe", tfn, reads=[hk, "ident"], writes=[pk])
                    for q in range(8):
                        kc = half * 8 + q
                        S.op("dve", lambda e, kc=kc, q=q, p16=p16, off=off, sz=sz: e.tensor_scalar(
                            hT[:, kc, off:off + sz], p16[:, q * 128:q * 128 + sz], pg_fm[:, kc:kc + 1], None, ALU.mult),
                            reads=[pk, "pg_fm"], writes=["hT"])

            if cfg.get('dbg') and phase_ctr[0] == 1:
                S.barrier(lambda e: e.memset(dummy[:, 0:1], 0.0))
                S.op('pool', lambda e: e.dma_start(out=dbg_hT, in_=hT.rearrange('p a b -> p (a b)')), dma=True)
                S.op('pool', lambda e: e.dma_start(out=dbg_x, in_=xin[0]), dma=True)
                S.op('pool', lambda e: e.dma_start(out=dbg_hb, in_=hbf[0]), dma=True)
                S.op('pool', lambda e: e.dma_start(out=dbg_sm, in_=small), dma=True)
            phase_reset()
            grT = A.alloc((512,), BF16)
            eb = A.alloc((4, 512), F32)
            ebinv = A.alloc((4, 512), BF16)
            tA = [A.alloc((512,), F32) for _ in range(2)]
            tB = [A.alloc((512,), F32) for _ in range(2)]
            qT = A.alloc((4, 512), BF16)
            kT = A.alloc((4, 512), BF16)
            gv16 = A.alloc((nb, 1024), BF16)
            sc16 = [A.alloc((128,), BF16) for _ in range(2)]
            khT = [A.alloc((128,), BF16) for _ in range(2)]
            kh = [A.alloc((128,), BF16) for _ in range(2)]
            sq16 = [A.alloc((512,), BF16) for _ in range(2)]
            gzs = [A.alloc((512,), BF16) for _ in range(2)]
            rstdT = [A.alloc((512,), F32) for _ in range(2)]
            if kind == "sample":
                S32s = [Arena2.alloc((4, 256), F32) for _ in range(2)]
                S16s = [Arena2.alloc((4, 256), BF16) for _ in range(2)]
            ridx = 1 if kind == "sample" else 0

            def h_gr(j, m, ps_ap, pk):
                S.op("act", lambda e: e.activation(grT[0:16, 0:n], ps_ap, AF.Copy), reads=[pk], writes=["grT"])
            proj_fm(l, "gr", n, h_gr)
            for h in range(4):
                pi = rot_lo.get()
                ps_ap, pk = PS[pi]
                mm_group(ps_ap[:, 0:n], pk, [(wa2_16[0:16, h * 128:(h + 1) * 128], grT[0:16, 0:n])], reads=["wa2", "grT"])
                a_, ak = tA[h % 2], ("tA", h % 2)
                b_, bk = tB[h % 2], ("tB", h % 2)
                S.op("act", lambda e, ps_ap=ps_ap, a_=a_, h=h: e.activation(a_[:, 0:n], ps_ap[:, 0:n], AF.Exp,
                                                                            bias=negba[:, h:h + 1], scale=-1.0),
                     reads=[pk, "negba"], writes=[ak])
                S.op("act", lambda e, a_=a_, b_=b_: e.activation(b_[:, 0:n], a_[:, 0:n], AF.Ln, bias=1.0, scale=1.0),
                     reads=[ak], writes=[bk])
                S.op("dve", lambda e, a_=a_, b_=b_: e.tensor_tensor_scan(a_[:, 0:n], rst[:, ridx, 0:n], b_[:, 0:n], 0.0,
                                                                         ALU.mult, ALU.add),
                     reads=[bk, "rst"], writes=[ak])
                S.op("act", lambda e, a_=a_, h=h: e.activation(eb[:, h, 0:n], a_[:, 0:n], AF.Exp, scale=-1.0 / 16.0),
                     reads=[ak], writes=[("eb", h)])
                S.op("act", lambda e, a_=a_, h=h: e.activation(ebinv[:, h, 0:n], a_[:, 0:n], AF.Exp, scale=1.0 / 16.0),
                     reads=[ak], writes=[("ebinv", h)])

            def h_gq(j, m, ps_ap, pk):
                S.op("dve", lambda e: e.scalar_tensor_tensor(qT[:, j, 0:n], ps_ap, 128.0 ** -0.5, eb[:, j, 0:n],
                                                             ALU.mult, ALU.mult),
                     reads=[pk, ("eb", j)], writes=[("qT", j)])
            proj_fm(l, "gq", n, h_gq)

            def h_gk(j, m, ps_ap, pk):
                S.op("dve", lambda e: e.tensor_tensor(kT[:, j, 0:n], ps_ap, ebinv[:, j, 0:n], ALU.mult),
                     reads=[pk, ("ebinv", j)], writes=[("kT", j)])
            proj_fm(l, "gk", n, h_gk)

            def h_gv(b, off, sz, g, ps_ap, pk):
                S.op("act", lambda e: e.activation(gv16[0:sz, b, g * G:(g + 1) * G], ps_ap, AF.Copy),
                     reads=[pk], writes=[("gv", b)])
            proj_tm(l, "gv", blocks, h_gv)

            for h in range(4):
                accs = [PS[4 + (h % 2) * 2 + j] for j in range(2)]
                for c, (off, sz) in enumerate(blocks):
                    if kind == "sample":
                        sl = c % 2
                        S32, S16 = S32s[sl], S16s[sl]
                        s32k, s16k = ("S32s", sl, h), ("S16s", sl, h)
                        sidx = tl["seqs"][c]
                        S.op("sp", lambda e, S32=S32, sidx=sidx, h=h: e.dma_start(out=S32[:, h, :], in_=sg[l, sidx, h]),
                             writes=[s32k], dma=True)
                        S.op("act", lambda e, S32=S32, S16=S16, h=h: e.activation(S16[:, h, :], S32[:, h, :], AF.Copy),
                             reads=[s32k], writes=[s16k])
                    else:
                        S32, S16 = S32p, S16p
                        s32k, s16k = ("S32p", h), ("S16p", h)
                        if tl.get("first") and c == 0:
                            S.op("pool", lambda e, h=h: e.memset(S32p[:, h, :], 0.0), writes=[s32k])
                            S.op("pool", lambda e, h=h: e.memset(S16p[:, h, :], 0.0), writes=[s16k])
                        if tl.get("init_from_meta") and c == 0:
                            S.op("sp", lambda e, h=h: e.dma_start(out=S32p[:, h, :], in_=smeta[h]), writes=[s32k], dma=True)
                            S.op("act", lambda e, h=h: e.activation(S16p[:, h, :], S32p[:, h, :], AF.Copy),
                                 reads=[s32k], writes=[s16k])
                    pi = rot_lo.get()
                    ps_ap, pk = PS[pi]
                    mm_group(ps_ap[0:sz, 0:sz], pk, [(kT[:, h, off:off + sz], qT[:, h, off:off + sz])],
                             reads=[("kT", h), ("qT", h)])
                    sc, sck = sc16[c % 2], ("sc16", c % 2)
                    S.op("dve", lambda e, sc=sc, ps_ap=ps_ap, sz=sz: e.tensor_tensor(sc[0:sz, 0:sz], ps_ap[0:sz, 0:sz],
                                                                                     mask32[0:sz, 385:385 + sz], ALU.mult),
                         reads=[pk, "mask32"], writes=[sck])
                    for j in range(2):
                        acc_ap, acck = accs[j]
                        mm_group(acc_ap[:, off:off + sz], acck,
                                 [(S16[:, h, j * 128:(j + 1) * 128], qT[:, h, off:off + sz]),
                                  (gv16[0:sz, c, h * 256 + j * 128: h * 256 + (j + 1) * 128], sc[0:sz, 0:sz])],
                                 reads=[s16k, ("qT", h), ("gv", c), sck])
                    kt_, ktk = khT[c % 2], ("khT", c % 2)
                    S.op("dve", lambda e, kt_=kt_, h=h, off=off, sz=sz: e.tensor_scalar(
                        kt_[:, 0:sz], kT[:, h, off:off + sz], eb[:, h, off + sz - 1:off + sz], None, ALU.mult),
                        reads=[("kT", h), ("eb", h)], writes=[ktk])
                    pi = rot_lo.get()
                    p16, pk2 = PS16[pi]
                    S.op("pe", lambda e, p16=p16, kt_=kt_, sz=sz: e.transpose(p16[0:sz, 0:128], kt_[:, 0:sz], ident16[:, :]),
                         reads=[ktk, "ident"], writes=[pk2])
                    kh_, khk = kh[c % 2], ("kh", c % 2)
                    S.op("act", lambda e, kh_=kh_, p16=p16, sz=sz: e.activation(kh_[0:sz, :], p16[0:sz, 0:128], AF.Copy),
                         reads=[pk2], writes=[khk])
                    pi = rot_lo.get()
                    ps3, pk3 = PS[pi]
                    mm_group(ps3[:, 0:256], pk3, [(kh_[0:sz, :], gv16[0:sz, c, h * 256:(h + 1) * 256])],
                             reads=[khk, ("gv", c)])
                    S.op("dve", lambda e, S32=S32, h=h, off=off, sz=sz, ps3=ps3: e.scalar_tensor_tensor(
                        S32[:, h, :], S32[:, h, :], eb[:, h, off + sz - 1:off + sz], ps3[:, 0:256], ALU.mult, ALU.add),
                        reads=[s32k, ("eb", h), pk3], writes=[s32k])
                    if kind == "sample":
                        S.op(STQ, lambda e, S32=S32, sidx=sidx, h=h: e.dma_start(out=gs[l, sidx, h], in_=S32[:, h, :]),
                             reads=[s32k], dma=True)
                    else:
                        S.op("act", lambda e, h=h: e.activation(S16p[:, h, :], S32p[:, h, :], AF.Copy),
                             reads=[s32k], writes=[s16k])
                        if tl.get("save_meta") and c == nb - 1:
                            S.op(STQ, lambda e, h=h: e.dma_start(out=smeta[h], in_=S32p[:, h, :]), reads=[s32k], dma=True)
                        if tl.get("final") and c == nb - 1:
                            bidx = tl["seq"]
                            S.op(STQ, lambda e, h=h, bidx=bidx: e.dma_start(out=gp[l, bidx, h], in_=S32p[:, h, :]),
                                 reads=[s32k], dma=True)
                for j in range(2):
                    acc_ap, acck = accs[j]
                    S.op("act", lambda e, j=j, acc_ap=acc_ap: e.activation(sq16[j][:, 0:n], acc_ap[:, 0:n], AF.Square),
                         reads=[acck], writes=[("sq16", j)])
                pi = rot_lo.get()
                psm, pkm = PS[pi]
                mm_group(psm[:, 0:n], pkm, [(onesmean16, sq16[0][:, 0:n]), (onesmean16, sq16[1][:, 0:n])],
                         reads=["onesmean", ("sq16", 0), ("sq16", 1)])
                rs, rsk = rstdT[h % 2], ("rstdT", h % 2)
                S.op("act", lambda e, rs=rs, psm=psm: e.activation(rs[:, 0:n], psm[:, 0:n], AF.Ln, bias=EPS, scale=1.0),
                     reads=[pkm], writes=[rsk])
                S.op("act", lambda e, rs=rs: e.activation(rs[:, 0:n], rs[:, 0:n], AF.Exp, scale=-0.5),
                     reads=[rsk], writes=[rsk])

                def h_gz(j, m, ps_ap, pk, h=h, rs=rs, rsk=rsk, accs=accs):
                    jj = j - 2 * h
                    gz_, gzk = gzs[jj], ("gzs", jj)
                    S.op("act", lambda e: e.activation(gz_[:, 0:n], ps_ap, AF.Silu), reads=[pk], writes=[gzk])
                    acc_ap, acck = accs[jj]
                    t_, tk = tB[jj], ("tB", jj)
                    S.op("dve", lambda e: e.scalar_tensor_tensor(t_[:, 0:n], acc_ap[:, 0:n], gg[:, j:j + 1], rs[:, 0:n],
                                                                 ALU.mult, ALU.mult),
                         reads=[acck, "gg", rsk], writes=[tk])
                    S.op("dve", lambda e: e.tensor_tensor(ogT[:, j, 0:n], t_[:, 0:n], gz_[:, 0:n], ALU.mult),
                         reads=[tk, gzk], writes=[("ogT", j)])
                proj_fm(l, "gz", n, h_gz, blocks=[2 * h, 2 * h + 1])

            phase_reset()
            sqT = A.alloc((8, 512), BF16)
            szs = A.alloc((8, 512), BF16)
            st32 = [A.alloc((G,), F32) for _ in range(4)]
            strot = Rot([(st32[i], ("st32", i)) for i in range(4)])
            ktm16 = A.alloc((nb, 1024), BF16)
            e32 = [A.alloc((512,), F32) for _ in range(3)]
            sp16 = [A.alloc((512,), BF16) for _ in range(3)]
            r32 = [A.alloc((512,), F32) for _ in range(2)]
            w16 = [A.alloc((512,), BF16) for _ in range(2)]
            Ls16 = A.alloc((512,), BF16)
            if kind == "sample":
                KTn = Arena2.alloc((8, 256), BF16)
                Vn = Arena2.alloc((nb, 1024), BF16)
                kst = [Arena2.alloc((1024,), F32) for _ in range(3)]
                vst = [Arena2.alloc((1024,), F32) for _ in range(3)]
                k16 = [Arena2.alloc((1024,), BF16) for _ in range(2)]
                v16 = [Arena2.alloc((1024,), BF16) for _ in range(6)]
                KTst = [Arena2.alloc((8, 128), BF16) for _ in range(5)]
                kstrot = Rot([(kst[i], ("kst", i)) for i in range(3)])
                vstrot = Rot([(vst[i], ("vst", i)) for i in range(3)])
                k16rot = Rot([(k16[i], ("k16", i)) for i in range(2)])
                v16rot = Rot([(v16[i], ("v16", i)) for i in range(6)])
                ktsrot = Rot([(KTst[i], ("KTst", i)) for i in range(5)])

            def h_sz(j, m, ps_ap, pk):
                S.op("act", lambda e: e.activation(szs[:, j, 0:n], ps_ap, AF.Silu), reads=[pk], writes=[("szs", j)])
            proj_fm(l, "sz", n, h_sz)

            def h_sq(j, m, ps_ap, pk):
                S.op("act", lambda e: e.activation(sqT[:, j, 0:n], ps_ap, AF.Copy, scale=128.0 ** -0.5),
                     reads=[pk], writes=[("sqT", j)])
            proj_fm(l, "sq", n, h_sq)

            def h_sk(b, off, sz, g, ps_ap, pk):
                s_, sk_ = strot.get()
                S.op("act", lambda e: e.activation(s_[0:sz, :], ps_ap, AF.Copy), reads=[pk], writes=[sk_])
                for dfull in tl["kout"][b]:
                    dst = dfull[:, g * G:(g + 1) * G]
                    S.op(STQ, lambda e, dst=dst: e.dma_start(out=dst, in_=s_[0:sz, :]), reads=[sk_], dma=True)
                S.op("dve", lambda e: e.tensor_scalar(ktm16[0:sz, b, g * G:(g + 1) * G], s_[0:sz, :], 1.0, None, ALU.mult), reads=[sk_],
                     writes=[("ktm", b)])
            proj_tm(l, "sk", blocks, h_sk)
            for b, (off, sz) in enumerate(blocks):
                pi = 6 + (b % 2)
                p16, pk = PS16[pi]

                def tfn(e, b=b, sz=sz, p16=p16):
                    ins = None
                    for h in range(8):
                        ins = e.transpose(p16[:, h * 128:h * 128 + sz], ktm16[0:sz, b, h * 128:(h + 1) * 128],
                                          ident16[0:sz, 0:sz])
                    return ins
                S.op("pe", tfn, reads=[("ktm", b), "ident"], writes=[pk])
                src = p16.rearrange("p (h k) -> p h k", k=128)[:, :, 0:sz]
                if kind == "sample":
                    dstT = KTn[:, :, off:off + sz]
                    wkey = ("KTn", b)
                else:
                    pos = tl["pos0"] + off
                    dstT = KT[:, :, pos:pos + sz]
                    wkey = ("KT", tl["kb0"] + b)
                S.op("dve", lambda e, dstT=dstT, src=src: e.tensor_scalar(dstT, src, 1.0, None, ALU.mult), reads=[pk], writes=[wkey])

            def h_sv(b, off, sz, g, ps_ap, pk):
                s_, sk_ = strot.get()
                S.op("act", lambda e: e.activation(s_[0:sz, :], ps_ap, AF.Copy), reads=[pk], writes=[sk_])
                for dfull in tl["vout"][b]:
                    dst = dfull[:, g * G:(g + 1) * G]
                    S.op(STQ, lambda e, dst=dst: e.dma_start(out=dst, in_=s_[0:sz, :]), reads=[sk_], dma=True)
                if kind == "sample":
                    S.op("dve", lambda e: e.tensor_scalar(Vn[0:sz, b, g * G:(g + 1) * G], s_[0:sz, :], 1.0, None, ALU.mult), reads=[sk_],
                         writes=[("Vn", b)])
                else:
                    S.op("dve", lambda e: e.tensor_scalar(Vc[0:sz, tl["kb0"] + b, g * G:(g + 1) * G], s_[0:sz, :], 1.0, None, ALU.mult), reads=[sk_],
                         writes=[("Vc", tl["kb0"] + b)])
            proj_tm(l, "sv", blocks, h_sv)

            att_ctr = [0]

            def attention(lanes, ncols, kblocks, finish):
                ai = att_ctr[0]
                att_ctr[0] += 1
                ops_ap, opsk = PS[4 + ai % 2]
                S.op("pool", lambda e: e.memset(Ls16[:, 0:ncols], 0.0), writes=["Ls16"])
                nkb = len(kblocks)

                def do_prep(i):
                    if i < nkb and kblocks[i].get("prep") is not None:
                        kblocks[i]["prep"](kblocks[i])

                def stageA1(bi):
                    kb = kblocks[bi]
                    nk = kb["nk"]
                    t = bi % 3
                    zi = rot_lo.get()
                    z_ap, zk = PS[zi]

                    def zfn(e):
                        ins = None
                        for li, (q_ap, qk, c0, w) in enumerate(lanes):
                            ins = e.matmul(z_ap[0:nk, c0:c0 + w], kb["kt"](li)[0], q_ap, start=True, stop=True)
                        return ins
                    rk = []
                    for li, (q_ap, qk, c0, w) in enumerate(lanes):
                        rk += [kb["kt"](li)[1], qk]
                    S.op("pe", zfn, reads=rk, writes=[zk])
                    e_, ek = e32[t], ("e32", t)
                    S.op("act", lambda e: e.activation(e_[0:nk, 0:ncols], z_ap[0:nk, 0:ncols], AF.Exp),
                         reads=[zk], writes=[ek])

                def stageA2(bi):
                    kb = kblocks[bi]
                    nk = kb["nk"]
                    t = bi % 3
                    e_, ek = e32[t], ("e32", t)
                    s_, sk_ = sp16[t], ("sp16", t)
                    S.op("act", lambda e: e.activation(s_[0:nk, 0:ncols], e_[0:nk, 0:ncols], AF.Ln, bias=1.0, scale=1.0),
                         reads=[ek], writes=[sk_])
                    if kb["m16"] is not None:
                        m16, m32, mkeys = kb["m16"], kb["m32"], kb["mkeys"]
                        S.op("pool", lambda e: e.tensor_tensor(s_[0:nk, 0:ncols], s_[0:nk, 0:ncols], m16, ALU.mult),
                             reads=[sk_] + mkeys, writes=[sk_])
                        S.op("pool", lambda e: e.tensor_tensor(e_[0:nk, 0:ncols], e_[0:nk, 0:ncols], m32, ALU.mult),
                             reads=[ek] + mkeys, writes=[ek])

                def stageB1a(bi):
                    kb = kblocks[bi]
                    nk = kb["nk"]
                    t = bi % 3
                    u = bi % 2
                    s_, sk_ = sp16[t], ("sp16", t)
                    ti = rot_lo.get()
                    t_ap, tk = PS[ti]
                    pairs = [(negtri16[0:nk, 0:nk], s_[0:nk, 0:ncols])]
                    rds = ["negtri", sk_]
                    if bi > 0:
                        pairs.append((negones16[:, 0:nk], Ls16[:, 0:ncols]))
                        rds += ["negones", "Ls16"]
                    mm_group(t_ap[0:nk, 0:ncols], tk, pairs, reads=rds)
                    r_, rk_ = r32[u], ("r32", u)
                    S.op("act", lambda e: e.activation(r_[0:nk, 0:ncols], t_ap[0:nk, 0:ncols], AF.Exp),
                         reads=[tk], writes=[rk_])

                def stageB1b(bi):
                    kb = kblocks[bi]
                    nk = kb["nk"]
                    t = bi % 3
                    u = bi % 2
                    e_, ek = e32[t], ("e32", t)
                    s_, sk_ = sp16[t], ("sp16", t)
                    r_, rk_ = r32[u], ("r32", u)
                    w_, wk_ = w16[u], ("w16", u)
                    S.op("dve", lambda e: e.tensor_tensor(w_[0:nk, 0:ncols], e_[0:nk, 0:ncols], r_[0:nk, 0:ncols], ALU.mult),
                         reads=[ek, rk_], writes=[wk_])
                    if bi < nkb - 1:
                        S.op("pool", lambda e: e.tensor_tensor(Ls16[0:nk, 0:ncols], Ls16[0:nk, 0:ncols], s_[0:nk, 0:ncols], ALU.add),
                             reads=[sk_, "Ls16"], writes=["Ls16"])

                def stageC(bi):
                    kb = kblocks[bi]
                    nk = kb["nk"]
                    u = bi % 2
                    w_, wk_ = w16[u], ("w16", u)

                    def ofn(e):
                        ins = None
                        for li, (q_ap, qk, c0, w) in enumerate(lanes):
                            ins = e.matmul(ops_ap[:, c0:c0 + w], kb["v"](li)[0], w_[0:nk, c0:c0 + w],
                                           start=(bi == 0 and li == 0), stop=(bi == nkb - 1))
                        return ins
                    rk = [wk_] + [kb["v"](li)[1] for li in range(len(lanes))]
                    S.op("pe", ofn, reads=rk, writes=[opsk])

                LOOK = 4
                for i in range(min(LOOK, nkb)):
                    do_prep(i)
                for i in range(min(2, nkb)):
                    stageA1(i)
                    stageA2(i)
                for bi in range(nkb):
                    do_prep(bi + LOOK)
                    if bi + 2 < nkb:
                        stageA1(bi + 2)
                    stageB1a(bi)
                    if bi + 2 < nkb:
                        stageA2(bi + 2)
                    stageB1b(bi)
                    if bi >= 1:
                        stageC(bi - 1)
                stageC(nkb - 1)
                finish(ops_ap, opsk)

            if kind != "sample":
                kb0 = tl["kb0"]
                for h in range(8):
                    kbl = []
                    for b in reversed(range(nb)):
                        off, sz = blocks[b]
                        pos = tl["pos0"] + off
                        kbl.append(dict(
                            nk=sz,
                            kt=lambda li, pos=pos, sz=sz, b=b, h=h: (KT[:, h, pos:pos + sz], ("KT", kb0 + b)),
                            v=lambda li, b=b, sz=sz, h=h: (Vc[0:sz, kb0 + b, h * 128:(h + 1) * 128], ("Vc", kb0 + b)),
                            m16=mask16[0:sz, 384 - off:384 - off + n], m32=mask32[0:sz, 384 - off:384 - off + n],
                            mkeys=["mask16", "mask32"]))
                    for kb in reversed(range(kb0)):
                        if kb == 0:
                            pos, sz = 0, N_META
                        else:
                            pos, sz = N_META + (kb - 1) * 128, 128
                        kbl.append(dict(
                            nk=sz,
                            kt=lambda li, pos=pos, sz=sz, kb=kb, h=h: (KT[:, h, pos:pos + sz], ("KT", kb)),
                            v=lambda li, kb=kb, sz=sz, h=h: (Vc[0:sz, kb, h * 128:(h + 1) * 128], ("Vc", kb)),
                            m16=None, m32=None, mkeys=None))

                    def fin(ops_ap, opsk, h=h):
                        S.op("dve", lambda e: e.tensor_tensor(osbT[:, h, 0:n], ops_ap[:, 0:n], szs[:, h, 0:n], ALU.mult),
                             reads=[opsk, ("szs", h)], writes=[("osbT", h)])
                    attention([(sqT[:, h, 0:n], ("sqT", h), 0, n)], n, kbl, fin)
            else:
                for c, (off, sz) in enumerate(blocks):
                    sidx = tl["seqs"][c]
                    lanes = [(sqT[:, h, off:off + sz], ("sqT", h), h * 64, 64) for h in range(8)]
                    kbl = [dict(
                        nk=sz,
                        kt=lambda li, off=off, sz=sz, c=c: (KTn[:, li, off:off + sz], ("KTn", c)),
                        v=lambda li, c=c, sz=sz: (Vn[0:sz, c, li * 128:(li + 1) * 128], ("Vn", c)),
                        m16=masks16[0:sz, :, :].rearrange("p a b -> p (a b)"),
                        m32=masks32[0:sz, :, :].rearrange("p a b -> p (a b)"),
                        mkeys=["masks16", "masks32"])]
                    for pb in reversed(range(NPB)):
                        def prep(kb, pb=pb, sidx=sidx):
                            k_, kk = kstrot.get()
                            v_, vk = vstrot.get()
                            kb_, kbk = k16rot.get()
                            vb_, vbk = v16rot.get()
                            kt_, ktk = ktsrot.get()
                            S.op("sp", lambda e: e.dma_start(out=k_, in_=ck[l, sidx, pb * 128:(pb + 1) * 128, :]),
                                 writes=[kk], dma=True)
                            S.op("sp", lambda e: e.dma_start(out=v_, in_=cv[l, sidx, pb * 128:(pb + 1) * 128, :]),
                                 writes=[vk], dma=True)
                            S.op("dve", lambda e: e.tensor_scalar(kb_, k_, 1.0, None, ALU.mult), reads=[kk], writes=[kbk])
                            S.op("pool", lambda e: e.tensor_copy(vb_, v_), reads=[vk], writes=[vbk])
                            pi = 6 + (pb % 2)
                            p16, pk = PS16[pi]

                            def tfn(e):
                                ins = None
                                for h in range(8):
                                    ins = e.transpose(p16[:, h * 128:(h + 1) * 128], kb_[:, h * 128:(h + 1) * 128], ident16)
                                return ins
                            S.op("pe", tfn, reads=[kbk, "ident"], writes=[pk])
                            S.op("dve", lambda e: e.tensor_scalar(kt_.rearrange("p a b -> p (a b)"), p16, 1.0, None, ALU.mult),
                                 reads=[pk], writes=[ktk])
                            kb["kt"] = lambda li: (kt_[:, li, :], ktk)
                            kb["v"] = lambda li: (vb_[:, li * 128:(li + 1) * 128], vbk)
                        kbl.append(dict(nk=128, prep=prep, m16=None, m32=None, mkeys=None))

                    def fin(ops_ap, opsk, off=off, sz=sz):
                        for h in range(8):
                            S.op("dve", lambda e, h=h: e.tensor_tensor(osbT[:, h, off:off + sz], ops_ap[:, h * 64:h * 64 + sz],
                                                                       szs[:, h, off:off + sz], ALU.mult),
                                 reads=[opsk, ("szs", h)], writes=[("osbT", h)])
                    attention(lanes, 512, kbl, fin)

            phase_reset()
            mrg = A.alloc((KC, 512), BF16)
            top_mrg = A.top
            sgt = [A.alloc((512,), F32) for _ in range(4)]
            for jp in range(8):
                wg, wgk = load_w(l, "in", SEC["mg"][0] + jp * G, G, KC)
                ws, wsk = load_w(l, "in", SEC["ms"][0] + jp * G, G, KC)
                wug, wugk = load_w(l, "ug", jp * G, G, 8)
                wus, wusk = load_w(l, "us", jp * G, G, 8)
                for jj in range(2):
                    j = jp * 2 + jj
                    o0 = jj * 128
                    base = (j % 2) * 4
                    (p_mg, k_mg), (p_ms, k_ms), (p_yg, k_yg), (p_ys, k_ys) = [PS[base + q] for q in range(4)]
                    mm_group(p_mg[:, 0:n], k_mg, [(wg[:, kc, o0:o0 + 128], hT[:, kc, 0:n]) for kc in range(KC)],
                             reads=[wgk, "hT"])
                    mm_group(p_ms[:, 0:n], k_ms, [(ws[:, kc, o0:o0 + 128], hT[:, kc, 0:n]) for kc in range(KC)],
                             reads=[wsk, "hT"])
                    mm_group(p_yg[:, 0:n], k_yg, [(wug[:, kc, o0:o0 + 128], ogT[:, kc, 0:n]) for kc in range(8)],
                             reads=[wugk] + [("ogT", q) for q in range(8)])
                    mm_group(p_ys[:, 0:n], k_ys, [(wus[:, kc, o0:o0 + 128], osbT[:, kc, 0:n]) for kc in range(8)],
                             reads=[wusk] + [("osbT", q) for q in range(8)])
                    g1, g1k = sgt[(j % 2) * 2], ("sgt", (j % 2) * 2)
                    g2, g2k = sgt[(j % 2) * 2 + 1], ("sgt", (j % 2) * 2 + 1)
                    S.op("act", lambda e, g1=g1, p_mg=p_mg: e.activation(g1[:, 0:n], p_mg[:, 0:n], AF.Sigmoid),
                         reads=[k_mg], writes=[g1k])
                    S.op("act", lambda e, g2=g2, p_ms=p_ms: e.activation(g2[:, 0:n], p_ms[:, 0:n], AF.Sigmoid),
                         reads=[k_ms], writes=[g2k])
                    S.op("dve", lambda e, g1=g1, p_yg=p_yg: e.tensor_tensor(g1[:, 0:n], g1[:, 0:n], p_yg[:, 0:n], ALU.mult),
                         reads=[g1k, k_yg], writes=[g1k])
                    S.op("dve", lambda e, g2=g2, p_ys=p_ys: e.tensor_tensor(g2[:, 0:n], g2[:, 0:n], p_ys[:, 0:n], ALU.mult),
                         reads=[g2k, k_ys], writes=[g2k])
                    S.op("pool", lambda e, g1=g1, g2=g2, j=j: e.tensor_tensor(mrg[:, j, 0:n], g1[:, 0:n], g2[:, 0:n], ALU.add),
                         reads=[g1k, g2k], writes=["mrg"])

            S.barrier(lambda e: e.memset(dummy[:, 0:1], 0.0))
            A.top = top_mrg
            u32 = u32_alias
            xin2 = [A.alloc((D,), F32) for _ in range(2)]
            junk = A.alloc((D,), BF16)
            for g in range(D // G):
                wo, wok = load_w(l, "o", g * G, G, KC)
                for b, (off, sz) in enumerate(blocks):
                    pi = rot_all.get()
                    ps_ap, pk = PS[pi]
                    mm_group(ps_ap[0:sz, 0:G], pk, [(mrg[:, kc, off:off + sz], wo[:, kc, 0:G]) for kc in range(KC)],
                             reads=[wok, "mrg"])
                    S.op("act", lambda e, b=b, sz=sz, g=g, ps_ap=ps_ap: e.activation(u32[0:sz, b, g * G:(g + 1) * G],
                                                                                     ps_ap[0:sz, 0:G], AF.Copy),
                         reads=[pk], writes=[("u32", b)])
            for b, (off, sz) in enumerate(blocks):
                sm, smk = small_rot.get()
                xi, xk = xin2[b % 2], ("xin2", b % 2)
                src = tl["xsrc"][b]
                S.op("sp", lambda e, xi=xi, sz=sz, src=src: e.dma_start(out=xi[0:sz, :], in_=src),
                     reads=tl["xsrc_keys"][b], writes=[xk], dma=True)
                S.op("act", lambda e, b=b, sz=sz, sm=sm: e.activation(junk[0:sz, :], u32[0:sz, b, :], AF.Square,
                                                                      accum_out=sm[0:sz, 0:1]),
                     reads=[("u32", b)], writes=["junk", smk])
                rstd_from_ssq(sm[0:sz, 0:1], smk, sz, 1.0 / D)
                S.op("dve", lambda e, b=b, sz=sz, sm=sm: e.scalar_tensor_tensor(u32[0:sz, b, :], u32[0:sz, b, :], sm[0:sz, 0:1],
                                                                                pgb[0:sz, :], ALU.mult, ALU.mult),
                     reads=[("u32", b), smk, "pgb"], writes=[("u32", b)])
                S.op("pool", lambda e, b=b, sz=sz, xi=xi: e.tensor_tensor(xi[0:sz, :], xi[0:sz, :], u32[0:sz, b, :], ALU.add),
                     reads=[("u32", b), xk], writes=[xk])
                for (dst, dkey) in tl["ysink"][b]:
                    S.op(STQ, lambda e, dst=dst, xi=xi, sz=sz: e.dma_start(out=dst, in_=xi[0:sz, :]),
                         reads=[xk], writes=dkey, dma=True)

        convert_layer(0)
        conv_some(10 ** 6)
        for l in (range(DEPTH) if cfg.get('stop') != 0 else []):
            S.barrier(lambda e: e.memset(dummy[:, 0:1], 0.0))
            if cfg.get('stop') is not None and l > 0:
                break
            S.op("sp", lambda e, l=l: e.dma_start(out=pg_fm, in_=pre_g[l]), writes=["pg_fm"], dma=True)
            S.op("sp", lambda e, l=l: e.dma_start(out=pgb, in_=post_g[l]), writes=["pgb"], dma=True)
            S.op("sp", lambda e, l=l: e.dma_start(out=gg, in_=gla_g[l]), writes=["gg"], dma=True)
            S.op("sp", lambda e, l=l: e.dma_start(out=negba, in_=b_a[l]), writes=["negba"], dma=True)
            S.op("act", lambda e: e.activation(negba, negba, AF.Copy, scale=-1.0), reads=["negba"], writes=["negba"])
            S.op("pool", lambda e, l=l: e.dma_start(out=wa2_16[0:16, :], in_=w_a2[l]), writes=["wa2"], dma=True)
            if l + 1 < DEPTH:
                convert_layer(l + 1)
            conv_per_tile = -(-len(conv_q) // (NP * NT))
            for b in range(NP):
                tiles = [(0, N_META)] + [(N_META + 512 * m, 512) for m in range(NT)]
                for ti, (t0, n) in enumerate(tiles):
                    if ti == 0 and b > 0:
                        continue
                    blocks = [(o, min(128, n - o)) for o in range(0, n, 128)]
                    xsrc, xkeys, ysink, kout, vout = [], [], [], [], []
                    for (off, sz) in blocks:
                        r0 = t0 + off
                        if l == 0:
                            if ti == 0:
                                xsrc.append(meta[0:sz, :])
                            else:
                                xsrc.append(xp[b, r0 - N_META:r0 - N_META + sz, :])
                            xkeys.append([])
                        else:
                            xsrc.append(yscr_p[b, r0:r0 + sz, :])
                            xkeys.append([("yscr_p", b, r0)])
                        sinks = []
                        if l < DEPTH - 1:
                            sinks.append((yscr_p[b, r0:r0 + sz, :], [("yscr_p", b, r0)]))
                        elif ti > 0:
                            sinks.append((yp[b, r0 - N_META:r0 - N_META + sz, :], []))
                        ysink.append(sinks)
                        if ti == 0:
                            kout.append([kp[l, bb, r0:r0 + sz, :] for bb in range(NP)])
                            vout.append([vp[l, bb, r0:r0 + sz, :] for bb in range(NP)])
                        else:
                            kout.append([kp[l, b, r0:r0 + sz, :]])
                            vout.append([vp[l, b, r0:r0 + sz, :]])
                    tl = dict(kind="prompt", n=n, blocks=blocks, xsrc=xsrc, xsrc_keys=xkeys, ysink=ysink,
                              kout=kout, vout=vout, pos0=t0, kb0=(0 if ti == 0 else 1 + 4 * (ti - 1)),
                              first=(ti == 0), final=(ti == len(tiles) - 1), seq=b,
                              save_meta=(ti == 0 and NP > 1), init_from_meta=(ti == 1 and b > 0))
                    run_tile(l, tl)
                    conv_some(conv_per_tile)
            conv_some(10 ** 6)
            for s0 in range(0, NS, 4):
                seqs = list(range(s0, min(NS, s0 + 4)))
                blocks = [(64 * i, 64) for i in range(len(seqs))]
                xsrc, xkeys, ysink, kout, vout = [], [], [], [], []
                for s in seqs:
                    if l == 0:
                        xsrc.append(xs[s])
                        xkeys.append([])
                    else:
                        xsrc.append(yscr_s[s])
                        xkeys.append([("yscr_s", s)])
                    if l < DEPTH - 1:
                        ysink.append([(yscr_s[s], [("yscr_s", s)])])
                    else:
                        ysink.append([(ys[s], [])])
                    kout.append([ksn[l, s]])
                    vout.append([vsn[l, s]])
                tl = dict(kind="sample", n=64 * len(seqs), blocks=blocks, xsrc=xsrc, xsrc_keys=xkeys, ysink=ysink,
                          kout=kout, vout=vout, seqs=seqs)
                run_tile(l, tl)
        S.emit(nc, st)
    return nc


def make_consts():
    p = np.arange(128)[:, None]
    g = np.arange(896)[None, :]
    mask = ((g - p - 384) > 0).astype(np.float32)
    j = np.arange(128)[:, None]
    s = np.arange(128)[None, :]
    negtri = -(j >= s).astype(np.float32)
    rst = np.ones((2, 128, 512), np.float32)
    rst[0, :, ::128] = 0.0
    rst[1, :, ::64] = 0.0
    return dict(c_ident=np.eye(128, dtype=np.float32), c_negtri=negtri, c_mask=mask, c_rst=rst)


def run_cfg(cfg, n_cores, inputs):
    NP, NS, DEPTH = cfg["NP"], cfg["NS"], cfg["DEPTH"]
    f = lambda a: np.ascontiguousarray(np.asarray(a, dtype=np.float32))
    nc = build_program(cfg)
    consts = make_consts()
    shared = dict(
        meta=f(inputs["meta_tokens"]),
        pre_g=f(np.asarray(inputs["pre_gain"]).reshape(DEPTH, 16, 128).transpose(0, 2, 1)),
        post_g=f(np.broadcast_to(np.asarray(inputs["post_gain"])[:, None, :], (DEPTH, 128, D))),
        w_in=f(inputs["w_in"]), w_a2=f(inputs["w_a2"]),
        b_a=f(np.asarray(inputs["b_a"]).reshape(DEPTH, 4, 128).transpose(0, 2, 1)),
        gla_g=f(np.asarray(inputs["gla_gain"]).reshape(DEPTH, 8, 128).transpose(0, 2, 1)),
        w_ug=f(inputs["w_up_gla"]), w_us=f(inputs["w_up_sb"]), w_o=f(inputs["w_o"]), **consts)
    in_maps = []
    for c in range(n_cores):
        m = dict(shared)
        m["xp"] = f(inputs["x_prompt"][c * NP:(c + 1) * NP])
        m["xs"] = f(inputs["x_sample"][c * NS:(c + 1) * NS])
        m["ck"] = f(np.asarray(inputs["cache_sb_k"])[:, c * NS:(c + 1) * NS].reshape(DEPTH, NS, cfg["PAST"], 1024))
        m["cv"] = f(np.asarray(inputs["cache_sb_v"])[:, c * NS:(c + 1) * NS].reshape(DEPTH, NS, cfg["PAST"], 1024))
        m["sg"] = f(np.asarray(inputs["state_gla"])[:, c * NS:(c + 1) * NS])
        in_maps.append(m)
    res = run_bass_kernel_spmd(nc, in_maps, core_ids=list(range(n_cores)))
    R = res.results
    if cfg.get('dbg'):
        return R
    T = N_META + cfg["SEQ"]
    cat = lambda k, ax: np.concatenate([np.asarray(r[k]) for r in R], axis=ax)
    y_prompt = cat("yp", 0)
    y_sample = cat("ys", 0)
    k_p = cat("kp", 1).reshape(DEPTH, n_cores * NP, T, 8, 128)
    v_p = cat("vp", 1).reshape(DEPTH, n_cores * NP, T, 8, 128)
    g_p = cat("gp", 1)
    k_s = cat("ksn", 1).reshape(DEPTH, n_cores * NS, 64, 8, 128)
    v_s = cat("vsn", 1).reshape(DEPTH, n_cores * NS, 64, 8, 128)
    g_s = cat("gs", 1)
    return (y_prompt, y_sample, k_p, v_p, g_p, k_s, v_s, g_s)


def kernel(x_prompt, x_sample, cache_sb_k, cache_sb_v, state_gla, meta_tokens, pre_gain, w_in, w_a2,
           b_a, gla_gain, w_up_gla, w_up_sb, w_o, post_gain):
    cfg = dict(NP=2, SEQ=2048, NS=4, PAST=4096, DEPTH=2)
    inputs = dict(x_prompt=np.asarray(x_prompt), x_sample=np.asarray(x_sample), cache_sb_k=cache_sb_k,
                  cache_sb_v=cache_sb_v, state_gla=state_gla, meta_tokens=meta_tokens, pre_gain=pre_gain,
                  w_in=w_in, w_a2=w_a2, b_a=b_a, gla_gain=gla_gain, w_up_gla=w_up_gla, w_up_sb=w_up_sb,
                  w_o=w_o, post_gain=post_gain)
    return run_cfg(cfg, 8, inputs)
```

```python
from contextlib import ExitStack
import numpy as np
import concourse.bass as bass
import concourse.mybir as mybir
from concourse.bass_utils import run_bass_kernel_spmd

F32 = mybir.dt.float32
BF16 = mybir.dt.bfloat16
AF = mybir.ActivationFunctionType
ALU = mybir.AluOpType

D = 2048
KC = 16
N_META = 16
EPS = 1e-6
IN_COLS = 11280
SEC = dict(gq=(0, 512), gk=(512, 512), gv=(1024, 1024), gz=(2048, 1024), gr=(3072, 16),
           sq=(3088, 1024), sk=(4112, 1024), sv=(5136, 1024), sz=(6160, 1024),
           mg=(7184, 2048), ms=(9232, 2048))
G = 256
NW = 4
ENGS = ("pe", "act", "dve", "pool", "sp")
N_DMA_SEMS = 8
STQ = "sp"


class Op:
    __slots__ = ("eng", "fn", "deps", "is_dma", "sem", "val", "signal", "idx")

    def __init__(self, eng, fn, is_dma):
        self.eng = eng
        self.fn = fn
        self.deps = []
        self.is_dma = is_dma
        self.sem = None
        self.val = None
        self.signal = False
        self.idx = None


class Sched:
    def __init__(self):
        self.ops = []
        self.last_w = {}
        self.readers = {}
        self.fence = None
        self.last_eng = {}
        self.dmas_since = []
        self.muted = False
        self.maxops = None

    def op(self, eng, fn, reads=(), writes=(), dma=False, nobar=False, nofence=False):
        if self.muted or (self.maxops is not None and len(self.ops) >= self.maxops):
            return None
        o = Op(eng, fn, dma)
        o.idx = len(self.ops)
        deps = set()
        for k in reads:
            w = self.last_w.get(k)
            if w is not None:
                deps.add(w)
        for k in writes:
            w = self.last_w.get(k)
            if w is not None:
                deps.add(w)
            for r in self.readers.get(k, ()):
                deps.add(r)
        for k in writes:
            self.last_w[k] = o
            self.readers[k] = []
        for k in reads:
            self.readers.setdefault(k, []).append(o)
        if self.fence is not None and not nofence:
            deps.add(self.fence)
        deps.discard(o)
        o.deps = sorted(deps, key=lambda d: d.idx)
        self.ops.append(o)
        if dma:
            if not nobar:
                self.dmas_since.append(o)
        else:
            self.last_eng[eng] = o
        return o

    def barrier(self, fn):
        if self.muted:
            return None
        o = Op("pool", fn, False)
        o.idx = len(self.ops)
        deps = set(self.last_eng.values()) | set(self.dmas_since)
        if self.fence is not None:
            deps.add(self.fence)
        o.deps = sorted(deps, key=lambda d: d.idx)
        self.ops.append(o)
        self.fence = o
        self.dmas_since = []
        self.last_eng = {"pool": o}
        self.last_w = {k: v for k, v in self.last_w.items()
                       if isinstance(k, tuple) and (str(k[0]).startswith('wb_') or k[0] == 'W')}
        self.readers = {k: v for k, v in self.readers.items() if isinstance(k, tuple) and k[0] == 'W'}
        return o

    def emit(self, nc, stack):
        engines = {"pe": "tensor", "act": "scalar", "dve": "vector", "pool": "gpsimd", "sp": "sync"}
        csem = {e: stack.enter_context(nc.semaphore("cs_" + e)) for e in ENGS}
        dsem = {e: [stack.enter_context(nc.semaphore("ds_%s_%d" % (e, i))) for i in range(N_DMA_SEMS)]
                for e in ("act", "pool", "sp")}
        pos = {}
        cnt = {e: 0 for e in ENGS}
        for o in self.ops:
            pos[o] = cnt[o.eng]
            cnt[o.eng] += 1

        def need_sync(o, d):
            if d.is_dma or d.eng != o.eng:
                return True
            return o.eng != "pe" and (pos[o] - pos[d]) <= 4

        for o in self.ops:
            for d in o.deps:
                if need_sync(o, d):
                    d.signal = True
        ccount = {e: 0 for e in ENGS}
        dcount = {e: [0] * N_DMA_SEMS for e in dsem}
        dnext = {e: 0 for e in dsem}
        dprev = {}
        for o in self.ops:
            if o.is_dma:
                e = o.eng
                i = dnext[e]
                dnext[e] = (i + 1) % N_DMA_SEMS
                dcount[e][i] += 16
                o.sem = dsem[e][i]
                o.val = dcount[e][i]
                p = dprev.get((e, i))
                if p is not None and p not in o.deps:
                    o.deps.append(p)
                dprev[(e, i)] = o
            elif o.signal:
                ccount[o.eng] += 1
                o.sem = csem[o.eng]
                o.val = ccount[o.eng]
        per_eng = {e: [o for o in self.ops if o.eng == e] for e in ENGS}
        block = stack.enter_context(nc.Block())

        def make(e):
            def body(eng):
                waited = {}
                for o in per_eng[e]:
                    for d in o.deps:
                        if not need_sync(o, d):
                            continue
                        key = d.sem.num
                        if waited.get(key, 0) >= d.val:
                            continue
                        eng.wait_ge(d.sem, d.val)
                        waited[key] = d.val
                    ins = o.fn(eng)
                    if o.is_dma:
                        ins.then_inc(o.sem, 16)
                    elif o.signal:
                        ins.then_inc(o.sem, 1)
                if e in dsem:
                    for i, s in enumerate(dsem[e]):
                        if dcount[e][i] > 0 and waited.get(s.num, 0) < dcount[e][i]:
                            eng.wait_ge(s, dcount[e][i])
            return body

        for e in ENGS:
            if per_eng[e]:
                getattr(block, engines[e])(make(e))


class Arena:
    def __init__(self, nc, st, nbytes):
        self.cap = nbytes
        self.t = st.enter_context(nc.sbuf_tensor("arena", [128, nbytes // 2], BF16))
        self.top = 0

    def alloc(self, shape, dt):
        n = 1
        for s in shape:
            n *= s
        isz = 4 if dt == F32 else 2
        off = (self.top + 3) // 4 * 4
        self.top = off + n * isz
        assert self.top <= self.cap, "arena overflow %d > %d" % (self.top, self.cap)
        ap = self.t[:, off // 2: off // 2 + n * isz // 2]
        if dt == F32:
            ap = ap.bitcast(F32)
        if len(shape) == 2:
            ap = ap.rearrange("p (a b) -> p a b", b=shape[1])
        elif len(shape) == 3:
            ap = ap.rearrange("p (a b c) -> p a b c", b=shape[1], c=shape[2])
        return ap


class Rot:
    def __init__(self, items):
        self.items = items
        self.i = 0

    def get(self):
        it = self.items[self.i]
        self.i = (self.i + 1) % len(self.items)
        return it


def build_program(cfg):
    NP, SEQ, NS, PAST, DEPTH = cfg["NP"], cfg["SEQ"], cfg["NS"], cfg["PAST"], cfg["DEPTH"]
    T = N_META + SEQ
    assert SEQ % 512 == 0 and PAST % 128 == 0
    NT = SEQ // 512
    NKB = 1 + 4 * NT
    NPB = PAST // 128

    nc = bass.Bass("TRN2", target_bir_lowering=False)

    def din(name, shape):
        return nc.dram_tensor(name, list(shape), F32, kind="ExternalInput").ap()

    def dout(name, shape):
        return nc.dram_tensor(name, list(shape), F32, kind="ExternalOutput").ap()

    xp = din("xp", (NP, SEQ, D))
    xs = din("xs", (NS, 64, D))
    ck = din("ck", (DEPTH, NS, PAST, 1024))
    cv = din("cv", (DEPTH, NS, PAST, 1024))
    sg = din("sg", (DEPTH, NS, 4, 128, 256))
    meta = din("meta", (N_META, D))
    pre_g = din("pre_g", (DEPTH, 128, 16))
    post_g = din("post_g", (DEPTH, 128, D))
    w_in = din("w_in", (DEPTH, D, IN_COLS))
    w_a2 = din("w_a2", (DEPTH, 16, 512))
    b_a = din("b_a", (DEPTH, 128, 4))
    gla_g = din("gla_g", (DEPTH, 128, 8))
    w_ug = din("w_ug", (DEPTH, 1024, D))
    w_us = din("w_us", (DEPTH, 1024, D))
    w_o = din("w_o", (DEPTH, D, D))
    c_ident = din("c_ident", (128, 128))
    c_negtri = din("c_negtri", (128, 128))
    c_mask = din("c_mask", (128, 896))
    c_rst = din("c_rst", (2, 128, 512))

    yp = dout("yp", (NP, SEQ, D))
    ys = dout("ys", (NS, 64, D))
    kp = dout("kp", (DEPTH, NP, T, 1024))
    vp = dout("vp", (DEPTH, NP, T, 1024))
    gp = dout("gp", (DEPTH, NP, 4, 128, 256))
    ksn = dout("ksn", (DEPTH, NS, 64, 1024))
    vsn = dout("vsn", (DEPTH, NS, 64, 1024))
    gs = dout("gs", (DEPTH, NS, 4, 128, 256))

    wb_in = (nc.dram_tensor("wb_in", [DEPTH, D, IN_COLS], BF16, kind="ExternalOutput").ap() if cfg.get("dbg") else nc.dram_tensor("wb_in", [DEPTH, D, IN_COLS], BF16).ap())
    if cfg.get("dbg"):
        dbg_hT = nc.dram_tensor("dbg_hT", [128, KC * 512], F32, kind="ExternalOutput").ap()
        dbg_x = nc.dram_tensor("dbg_x", [128, D], F32, kind="ExternalOutput").ap()
        dbg_hb = nc.dram_tensor("dbg_hb", [128, D], F32, kind="ExternalOutput").ap()
        dbg_sm = nc.dram_tensor("dbg_sm", [128, 64], F32, kind="ExternalOutput").ap()
    wb_ug = nc.dram_tensor("wb_ug", [DEPTH, 1024, D], BF16).ap()
    wb_us = nc.dram_tensor("wb_us", [DEPTH, 1024, D], BF16).ap()
    wb_o = nc.dram_tensor("wb_o", [DEPTH, D, D], BF16).ap()
    yscr_p = nc.dram_tensor("yscr_p", [NP, T, D], F32).ap()
    yscr_s = nc.dram_tensor("yscr_s", [NS, 64, D], F32).ap()
    smeta = nc.dram_tensor("smeta", [4, 128, 256], F32).ap()

    S = Sched()
    S.maxops = cfg.get('maxops')
    st = ExitStack()
    with st:
        A = Arena(nc, st, 206 * 1024)
        psb = [st.enter_context(nc.psum_tensor("ps%d" % i, [128, 512], F32)) for i in range(8)]
        PS = [(psb[i][:], ("ps", i)) for i in range(8)]
        PS16 = [(psb[i][:].bitcast(BF16), ("ps", i)) for i in range(8)]
        rot_lo = Rot(list(range(4)))
        rot_all = Rot(list(range(8)))

        ident16 = A.alloc((128,), BF16)
        negtri16 = A.alloc((128,), BF16)
        negones16 = A.alloc((128,), BF16)
        onesmean16 = A.alloc((128,), BF16)
        mask16 = A.alloc((896,), BF16)
        mask32 = A.alloc((896,), F32)
        masks16 = A.alloc((8, 64), BF16)
        masks32 = A.alloc((8, 64), F32)
        rst = A.alloc((2, 512), F32)
        pg_fm = A.alloc((16,), F32)
        negba = A.alloc((4,), F32)
        gg = A.alloc((8,), F32)
        wa2_16 = A.alloc((512,), BF16)
        pgb = A.alloc((D,), F32)
        A.top = (A.top + 3) // 4 * 4
        u_off = A.top
        hT = A.alloc((KC, 512), BF16)
        ogT = A.alloc((8, 512), BF16)
        osbT = A.alloc((8, 512), BF16)
        u32_alias = A.t[:, u_off // 2: u_off // 2 + 4 * D * 2].bitcast(F32).rearrange("p (a b) -> p a b", b=D)
        Wb = [A.alloc((KC, G), BF16) for _ in range(NW)]
        wrot = Rot([(Wb[i], ("W", i)) for i in range(NW)])
        A.top = (A.top + 3) // 4 * 4
        kv_off = A.top
        KT = A.alloc((8, T), BF16)
        Vc = A.alloc((NKB, 1024), BF16)
        A.top = max(A.top, kv_off + 64 * 1024)
        kv_end = A.top
        S32p = A.alloc((4, 256), F32)
        S16p = A.alloc((4, 256), BF16)
        dummy = A.alloc((8,), F32)
        small = A.alloc((64,), F32)
        small_rot = Rot([(small[:, 4 * i: 4 * i + 4], ("small", i)) for i in range(16)])
        persist_top = A.top

        phase_ctr = [0]

        class Arena2:
            top = kv_off

            @staticmethod
            def alloc(shape, dt):
                save = A.top
                A.top = Arena2.top
                ap = A.alloc(shape, dt)
                Arena2.top = A.top
                assert Arena2.top <= kv_end, "arena2 overflow"
                A.top = save
                return ap

        def phase_reset():
            Arena2.top = kv_off
            phase_ctr[0] += 1
            if cfg.get('verbose'):
                print('phase', phase_ctr[0], 'ops', len(S.ops), 'arena top', A.top, 'persist', persist_top)
            if cfg.get('stop') is not None and phase_ctr[0] > cfg['stop']:
                S.muted = True
            S.barrier(lambda e: e.memset(dummy[:, 0:1], 0.0))
            A.top = persist_top

        S.op("pool", lambda e: e.dma_start(out=ident16, in_=c_ident), writes=["ident"], dma=True)
        S.op("pool", lambda e: e.dma_start(out=negtri16, in_=c_negtri), writes=["negtri"], dma=True)
        S.op("pool", lambda e: e.dma_start(out=mask16, in_=c_mask), writes=["mask16"], dma=True)
        S.op("sp", lambda e: e.dma_start(out=mask32, in_=c_mask), writes=["mask32"], dma=True)
        S.op("sp", lambda e: e.dma_start(out=rst, in_=c_rst.rearrange("a p n -> p a n")), writes=["rst"], dma=True)
        S.op("pool", lambda e: e.memset(negones16, -1.0), writes=["negones"])
        S.op("pool", lambda e: e.memset(onesmean16, 1.0 / 256.0), writes=["onesmean"])
        for h in range(8):
            S.op("pool", lambda e, h=h: e.tensor_copy(masks16[:, h, :], mask16[:, 384:448]),
                 reads=["mask16"], writes=["masks16"])
            S.op("pool", lambda e, h=h: e.tensor_copy(masks32[:, h, :], mask32[:, 384:448]),
                 reads=["mask32"], writes=["masks32"])

        conv_q = []

        def convert_layer(l):
            for c0 in range(0, IN_COLS, 512):
                c1 = min(IN_COLS, c0 + 512)
                conv_q.append((lambda e, c0=c0, c1=c1, l=l: e.dma_start(out=wb_in[l, :, c0:c1], in_=w_in[l, :, c0:c1]),
                               ("wb_in", l, c0 // 512)))
            for (dst, src, nm) in ((wb_ug, w_ug, "wb_ug"), (wb_us, w_us, "wb_us"), (wb_o, w_o, "wb_o")):
                for c0 in range(0, D, 512):
                    conv_q.append((lambda e, c0=c0, dst=dst, src=src, l=l: e.dma_start(out=dst[l, :, c0:c0 + 512],
                                                                                      in_=src[l, :, c0:c0 + 512]),
                                   (nm, l, c0 // 512)))

        def conv_some(k):
            for _ in range(min(k, len(conv_q))):
                fn, key = conv_q.pop(0)
                S.op("pool", fn, writes=[key], dma=True, nobar=True)

        def load_w(l, which, c0, ncols, kchunks):
            buf, key = wrot.get()
            if which == "in":
                src = wb_in[l, :, c0:c0 + ncols]
                rk = [("wb_in", l, c0 // 512), ("wb_in", l, (c0 + ncols - 1) // 512)]
            else:
                src = {"ug": wb_ug, "us": wb_us, "o": wb_o}[which][l, :, c0:c0 + ncols]
                rk = [("wb_" + which, l, c0 // 512)]
            src = src.rearrange("(kc p) g -> p kc g", p=128)
            dst = buf[:, 0:kchunks, 0:ncols]
            S.op("sp", lambda e: e.dma_start(out=dst, in_=src), reads=rk, writes=[key], dma=True, nofence=True)
            return dst, key

        def mm_group(ps_ap, pskey, pairs, reads):
            def fn(e):
                ins = None
                n = len(pairs)
                for i, (l_, r_) in enumerate(pairs):
                    ins = e.matmul(ps_ap, l_, r_, start=(i == 0), stop=(i == n - 1))
                return ins
            S.op("pe", fn, reads=reads, writes=[pskey])

        def proj_fm(l, sec, n, handler, blocks=None):
            c0, nc_ = SEC[sec]
            nblk = (nc_ + 127) // 128
            wcur = None
            for j in (blocks if blocks is not None else range(nblk)):
                gidx = (j * 128) // G
                if wcur is None or wcur[0] != gidx:
                    gc0 = c0 + gidx * G
                    gn = min(G, c0 + nc_ - gc0)
                    w_ap, w_key = load_w(l, "in", gc0, gn, KC)
                    wcur = (gidx, w_ap, w_key)
                _, w_ap, w_key = wcur
                m = min(128, nc_ - j * 128)
                o0 = j * 128 - gidx * G
                pi = rot_lo.get()
                ps_ap, ps_key = PS[pi]
                mm_group(ps_ap[0:m, 0:n], ps_key,
                         [(w_ap[:, kc, o0:o0 + m], hT[:, kc, 0:n]) for kc in range(KC)],
                         reads=[w_key, "hT"])
                handler(j, m, ps_ap[0:m, 0:n], ps_key)

        def proj_tm(l, sec, blocks, handler):
            c0, nc_ = SEC[sec]
            ng = nc_ // G
            nxt = load_w(l, "in", c0, G, KC)
            for g in range(ng):
                w_ap, w_key = nxt
                if g + 1 < ng:
                    nxt = load_w(l, "in", c0 + (g + 1) * G, G, KC)
                for b, (off, sz) in enumerate(blocks):
                    pi = rot_lo.get()
                    ps_ap, ps_key = PS[pi]
                    mm_group(ps_ap[0:sz, 0:G], ps_key,
                             [(hT[:, kc, off:off + sz], w_ap[:, kc, 0:G]) for kc in range(KC)],
                             reads=[w_key, "hT"])
                    handler(b, off, sz, g, ps_ap[0:sz, 0:G], ps_key)

        def rstd_from_ssq(ssq_ap, key, sz, scale):
            if cfg.get('dbg_norstd'):
                return
            S.op("act", lambda e: e.activation(ssq_ap, ssq_ap, AF.Ln, bias=EPS, scale=scale), reads=[key], writes=[key])
            S.op("act", lambda e: e.activation(ssq_ap, ssq_ap, AF.Exp, scale=-0.5), reads=[key], writes=[key])

        def run_tile(l, tl):
            kind = tl["kind"]
            n = tl["n"]
            blocks = tl["blocks"]
            nb = len(blocks)
            last = (l == DEPTH - 1)

            phase_reset()
            xin = [A.alloc((D,), F32) for _ in range(2)]
            hbf = [A.alloc((D,), BF16) for _ in range(2)]
            for b, (off, sz) in enumerate(blocks):
                xi, xk = xin[b % 2], ("xin", b % 2)
                hb, hk = hbf[b % 2], ("hbf", b % 2)
                sm, smk = small_rot.get()
                src = tl["xsrc"][b]
                S.op("sp", lambda e, xi=xi, sz=sz, src=src: e.dma_start(out=xi[0:sz, :], in_=src),
                     reads=tl["xsrc_keys"][b], writes=[xk], dma=True)
                S.op("act", lambda e, xi=xi, hb=hb, sm=sm, sz=sz: e.activation(hb[0:sz, :], xi[0:sz, :], AF.Square,
                                                                                accum_out=sm[0:sz, 0:1]),
                     reads=[xk], writes=[hk, smk])
                rstd_from_ssq(sm[0:sz, 0:1], smk, sz, 1.0 / D)
                S.op("dve", lambda e, xi=xi, hb=hb, sm=sm, sz=sz: e.tensor_scalar(hb[0:sz, :], xi[0:sz, :], sm[0:sz, 0:1],
                                                                                   None, ALU.mult),
                     reads=[xk, smk], writes=[hk])
                for half in range(2):
                    pi = rot_lo.get()
                    p16, pk = PS16[pi]

                    def tfn(e, hb=hb, sz=sz, half=half, p16=p16):
                        ins = None
                        for q in range(8):
                            kc = half * 8 + q
                            ins = e.transpose(p16[:, q * 128:q * 128 + sz], hb[0:sz, kc * 128:(kc + 1) * 128],
                                              ident16[0:sz, 0:sz])
                        return ins
                    S.op("pe", tfn, reads=[hk, "ident"], writes=[pk])
                    for q in range(8):
                        kc = half * 8 + q
                        S.op("dve", lambda e, kc=kc, q=q, p16=p16, off=off, sz=sz: e.tensor_scalar(
                            hT[:, kc, off:off + sz], p16[:, q * 128:q * 128 + sz], pg_fm[:, kc:kc + 1], None, ALU.mult),
                            reads=[pk, "pg_fm"], writes=["hT"])

            if cfg.get('dbg') and phase_ctr[0] == 1:
                S.barrier(lambda e: e.memset(dummy[:, 0:1], 0.0))
                S.op('pool', lambda e: e.dma_start(out=dbg_hT, in_=hT.rearrange('p a b -> p (a b)')), dma=True)
                S.op('pool', lambda e: e.dma_start(out=dbg_x, in_=xin[0]), dma=True)
                S.op('pool', lambda e: e.dma_start(out=dbg_hb, in_=hbf[0]), dma=True)
                S.op('pool', lambda e: e.dma_start(out=dbg_sm, in_=small), dma=True)
            phase_reset()
            grT = A.alloc((512,), BF16)
            eb = A.alloc((4, 512), F32)
            ebinv = A.alloc((4, 512), BF16)
            tA = [A.alloc((512,), F32) for _ in range(2)]
            tB = [A.alloc((512,), F32) for _ in range(2)]
            qT = A.alloc((4, 512), BF16)
            kT = A.alloc((4, 512), BF16)
            gv16 = A.alloc((nb, 1024), BF16)
            sc16 = [A.alloc((128,), BF16) for _ in range(2)]
            khT = [A.alloc((128,), BF16) for _ in range(2)]
            kh = [A.alloc((128,), BF16) for _ in range(2)]
            sq16 = [A.alloc((512,), BF16) for _ in range(2)]
            gzs = [A.alloc((512,), BF16) for _ in range(2)]
            rstdT = [A.alloc((512,), F32) for _ in range(2)]
            if kind == "sample":
                S32s = [Arena2.alloc((4, 256), F32) for _ in range(2)]
                S16s = [Arena2.alloc((4, 256), BF16) for _ in range(2)]
            ridx = 1 if kind == "sample" else 0

            def h_gr(j, m, ps_ap, pk):
                S.op("act", lambda e: e.activation(grT[0:16, 0:n], ps_ap, AF.Copy), reads=[pk], writes=["grT"])
            proj_fm(l, "gr", n, h_gr)
            for h in range(4):
                pi = rot_lo.get()
                ps_ap, pk = PS[pi]
                mm_group(ps_ap[:, 0:n], pk, [(wa2_16[0:16, h * 128:(h + 1) * 128], grT[0:16, 0:n])], reads=["wa2", "grT"])
                a_, ak = tA[h % 2], ("tA", h % 2)
                b_, bk = tB[h % 2], ("tB", h % 2)
                S.op("act", lambda e, ps_ap=ps_ap, a_=a_, h=h: e.activation(a_[:, 0:n], ps_ap[:, 0:n], AF.Exp,
                                                                            bias=negba[:, h:h + 1], scale=-1.0),
                     reads=[pk, "negba"], writes=[ak])
                S.op("act", lambda e, a_=a_, b_=b_: e.activation(b_[:, 0:n], a_[:, 0:n], AF.Ln, bias=1.0, scale=1.0),
                     reads=[ak], writes=[bk])
                S.op("dve", lambda e, a_=a_, b_=b_: e.tensor_tensor_scan(a_[:, 0:n], rst[:, ridx, 0:n], b_[:, 0:n], 0.0,
                                                                         ALU.mult, ALU.add),
                     reads=[bk, "rst"], writes=[ak])
                S.op("act", lambda e, a_=a_, h=h: e.activation(eb[:, h, 0:n], a_[:, 0:n], AF.Exp, scale=-1.0 / 16.0),
                     reads=[ak], writes=[("eb", h)])
                S.op("act", lambda e, a_=a_, h=h: e.activation(ebinv[:, h, 0:n], a_[:, 0:n], AF.Exp, scale=1.0 / 16.0),
                     reads=[ak], writes=[("ebinv", h)])

            def h_gq(j, m, ps_ap, pk):
                S.op("dve", lambda e: e.scalar_tensor_tensor(qT[:, j, 0:n], ps_ap, 128.0 ** -0.5, eb[:, j, 0:n],
                                                             ALU.mult, ALU.mult),
                     reads=[pk, ("eb", j)], writes=[("qT", j)])
            proj_fm(l, "gq", n, h_gq)

            def h_gk(j, m, ps_ap, pk):
                S.op("dve", lambda e: e.tensor_tensor(kT[:, j, 0:n], ps_ap, ebinv[:, j, 0:n], ALU.mult),
                     reads=[pk, ("ebinv", j)], writes=[("kT", j)])
            proj_fm(l, "gk", n, h_gk)

            def h_gv(b, off, sz, g, ps_ap, pk):
                S.op("act", lambda e: e.activation(gv16[0:sz, b, g * G:(g + 1) * G], ps_ap, AF.Copy),
                     reads=[pk], writes=[("gv", b)])
            proj_tm(l, "gv", blocks, h_gv)

            for h in range(4):
                accs = [PS[4 + (h % 2) * 2 + j] for j in range(2)]
                for c, (off, sz) in enumerate(blocks):
                    if kind == "sample":
                        sl = c % 2
                        S32, S16 = S32s[sl], S16s[sl]
                        s32k, s16k = ("S32s", sl, h), ("S16s", sl, h)
                        sidx = tl["seqs"][c]
                        S.op("sp", lambda e, S32=S32, sidx=sidx, h=h: e.dma_start(out=S32[:, h, :], in_=sg[l, sidx, h]),
                             writes=[s32k], dma=True)
                        S.op("act", lambda e, S32=S32, S16=S16, h=h: e.activation(S16[:, h, :], S32[:, h, :], AF.Copy),
                             reads=[s32k], writes=[s16k])
                    else:
                        S32, S16 = S32p, S16p
                        s32k, s16k = ("S32p", h), ("S16p", h)
                        if tl.get("first") and c == 0:
                            S.op("pool", lambda e, h=h: e.memset(S32p[:, h, :], 0.0), writes=[s32k])
                            S.op("pool", lambda e, h=h: e.memset(S16p[:, h, :], 0.0), writes=[s16k])
                        if tl.get("init_from_meta") and c == 0:
                            S.op("sp", lambda e, h=h: e.dma_start(out=S32p[:, h, :], in_=smeta[h]), writes=[s32k], dma=True)
                            S.op("act", lambda e, h=h: e.activation(S16p[:, h, :], S32p[:, h, :], AF.Copy),
                                 reads=[s32k], writes=[s16k])
                    pi = rot_lo.get()
                    ps_ap, pk = PS[pi]
                    mm_group(ps_ap[0:sz, 0:sz], pk, [(kT[:, h, off:off + sz], qT[:, h, off:off + sz])],
                             reads=[("kT", h), ("qT", h)])
                    sc, sck = sc16[c % 2], ("sc16", c % 2)
                    S.op("dve", lambda e, sc=sc, ps_ap=ps_ap, sz=sz: e.tensor_tensor(sc[0:sz, 0:sz], ps_ap[0:sz, 0:sz],
                                                                                     mask32[0:sz, 385:385 + sz], ALU.mult),
                         reads=[pk, "mask32"], writes=[sck])
                    for j in range(2):
                        acc_ap, acck = accs[j]
                        mm_group(acc_ap[:, off:off + sz], acck,
                                 [(S16[:, h, j * 128:(j + 1) * 128], qT[:, h, off:off + sz]),
                                  (gv16[0:sz, c, h * 256 + j * 128: h * 256 + (j + 1) * 128], sc[0:sz, 0:sz])],
                                 reads=[s16k, ("qT", h), ("gv", c), sck])
                    kt_, ktk = khT[c % 2], ("khT", c % 2)
                    S.op("dve", lambda e, kt_=kt_, h=h, off=off, sz=sz: e.tensor_scalar(
                        kt_[:, 0:sz], kT[:, h, off:off + sz], eb[:, h, off + sz - 1:off + sz], None, ALU.mult),
                        reads=[("kT", h), ("eb", h)], writes=[ktk])
                    pi = rot_lo.get()
                    p16, pk2 = PS16[pi]
                    S.op("pe", lambda e, p16=p16, kt_=kt_, sz=sz: e.transpose(p16[0:sz, 0:128], kt_[:, 0:sz], ident16[:, :]),
                         reads=[ktk, "ident"], writes=[pk2])
                    kh_, khk = kh[c % 2], ("kh", c % 2)
                    S.op("act", lambda e, kh_=kh_, p16=p16, sz=sz: e.activation(kh_[0:sz, :], p16[0:sz, 0:128], AF.Copy),
                         reads=[pk2], writes=[khk])
                    pi = rot_lo.get()
                    ps3, pk3 = PS[pi]
                    mm_group(ps3[:, 0:256], pk3, [(kh_[0:sz, :], gv16[0:sz, c, h * 256:(h + 1) * 256])],
                             reads=[khk, ("gv", c)])
                    S.op("dve", lambda e, S32=S32, h=h, off=off, sz=sz, ps3=ps3: e.scalar_tensor_tensor(
                        S32[:, h, :], S32[:, h, :], eb[:, h, off + sz - 1:off + sz], ps3[:, 0:256], ALU.mult, ALU.add),
                        reads=[s32k, ("eb", h), pk3], writes=[s32k])
                    if kind == "sample":
                        S.op(STQ, lambda e, S32=S32, sidx=sidx, h=h: e.dma_start(out=gs[l, sidx, h], in_=S32[:, h, :]),
                             reads=[s32k], dma=True)
                    else:
                        S.op("act", lambda e, h=h: e.activation(S16p[:, h, :], S32p[:, h, :], AF.Copy),
                             reads=[s32k], writes=[s16k])
                        if tl.get("save_meta") and c == nb - 1:
                            S.op(STQ, lambda e, h=h: e.dma_start(out=smeta[h], in_=S32p[:, h, :]), reads=[s32k], dma=True)
                        if tl.get("final") and c == nb - 1:
                            bidx = tl["seq"]
                            S.op(STQ, lambda e, h=h, bidx=bidx: e.dma_start(out=gp[l, bidx, h], in_=S32p[:, h, :]),
                                 reads=[s32k], dma=True)
                for j in range(2):
                    acc_ap, acck = accs[j]
                    S.op("act", lambda e, j=j, acc_ap=acc_ap: e.activation(sq16[j][:, 0:n], acc_ap[:, 0:n], AF.Square),
                         reads=[acck], writes=[("sq16", j)])
                pi = rot_lo.get()
                psm, pkm = PS[pi]
                mm_group(psm[:, 0:n], pkm, [(onesmean16, sq16[0][:, 0:n]), (onesmean16, sq16[1][:, 0:n])],
                         reads=["onesmean", ("sq16", 0), ("sq16", 1)])
                rs, rsk = rstdT[h % 2], ("rstdT", h % 2)
                S.op("act", lambda e, rs=rs, psm=psm: e.activation(rs[:, 0:n], psm[:, 0:n], AF.Ln, bias=EPS, scale=1.0),
                     reads=[pkm], writes=[rsk])
                S.op("act", lambda e, rs=rs: e.activation(rs[:, 0:n], rs[:, 0:n], AF.Exp, scale=-0.5),
                     reads=[rsk], writes=[rsk])

                def h_gz(j, m, ps_ap, pk, h=h, rs=rs, rsk=rsk, accs=accs):
                    jj = j - 2 * h
                    gz_, gzk = gzs[jj], ("gzs", jj)
                    S.op("act", lambda e: e.activation(gz_[:, 0:n], ps_ap, AF.Silu), reads=[pk], writes=[gzk])
                    acc_ap, acck = accs[jj]
                    t_, tk = tB[jj], ("tB", jj)
                    S.op("dve", lambda e: e.scalar_tensor_tensor(t_[:, 0:n], acc_ap[:, 0:n], gg[:, j:j + 1], rs[:, 0:n],
                                                                 ALU.mult, ALU.mult),
                         reads=[acck, "gg", rsk], writes=[tk])
                    S.op("dve", lambda e: e.tensor_tensor(ogT[:, j, 0:n], t_[:, 0:n], gz_[:, 0:n], ALU.mult),
                         reads=[tk, gzk], writes=[("ogT", j)])
                proj_fm(l, "gz", n, h_gz, blocks=[2 * h, 2 * h + 1])

            phase_reset()
            sqT = A.alloc((8, 512), BF16)
            szs = A.alloc((8, 512), BF16)
            st32 = [A.alloc((G,), F32) for _ in range(4)]
            strot = Rot([(st32[i], ("st32", i)) for i in range(4)])
            ktm16 = A.alloc((nb, 1024), BF16)
            e32 = [A.alloc((512,), F32) for _ in range(3)]
            sp16 = [A.alloc((512,), BF16) for _ in range(3)]
            r32 = [A.alloc((512,), F32) for _ in range(2)]
            w16 = [A.alloc((512,), BF16) for _ in range(2)]
            Ls16 = A.alloc((512,), BF16)
            if kind == "sample":
                KTn = Arena2.alloc((8, 256), BF16)
                Vn = Arena2.alloc((nb, 1024), BF16)
                kst = [Arena2.alloc((1024,), F32) for _ in range(3)]
                vst = [Arena2.alloc((1024,), F32) for _ in range(3)]
                k16 = [Arena2.alloc((1024,), BF16) for _ in range(2)]
                v16 = [Arena2.alloc((1024,), BF16) for _ in range(6)]
                KTst = [Arena2.alloc((8, 128), BF16) for _ in range(5)]
                kstrot = Rot([(kst[i], ("kst", i)) for i in range(3)])
                vstrot = Rot([(vst[i], ("vst", i)) for i in range(3)])
                k16rot = Rot([(k16[i], ("k16", i)) for i in range(2)])
                v16rot = Rot([(v16[i], ("v16", i)) for i in range(6)])
                ktsrot = Rot([(KTst[i], ("KTst", i)) for i in range(5)])

            def h_sz(j, m, ps_ap, pk):
                S.op("act", lambda e: e.activation(szs[:, j, 0:n], ps_ap, AF.Silu), reads=[pk], writes=[("szs", j)])
            proj_fm(l, "sz", n, h_sz)

            def h_sq(j, m, ps_ap, pk):
                S.op("act", lambda e: e.activation(sqT[:, j, 0:n], ps_ap, AF.Copy, scale=128.0 ** -0.5),
                     reads=[pk], writes=[("sqT", j)])
            proj_fm(l, "sq", n, h_sq)

            def h_sk(b, off, sz, g, ps_ap, pk):
                s_, sk_ = strot.get()
                S.op("act", lambda e: e.activation(s_[0:sz, :], ps_ap, AF.Copy), reads=[pk], writes=[sk_])
                for dfull in tl["kout"][b]:
                    dst = dfull[:, g * G:(g + 1) * G]
                    S.op(STQ, lambda e, dst=dst: e.dma_start(out=dst, in_=s_[0:sz, :]), reads=[sk_], dma=True)
                S.op("dve", lambda e: e.tensor_scalar(ktm16[0:sz, b, g * G:(g + 1) * G], s_[0:sz, :], 1.0, None, ALU.mult), reads=[sk_],
                     writes=[("ktm", b)])
            proj_tm(l, "sk", blocks, h_sk)
            for b, (off, sz) in enumerate(blocks):
                pi = 6 + (b % 2)
                p16, pk = PS16[pi]

                def tfn(e, b=b, sz=sz, p16=p16):
                    ins = None
                    for h in range(8):
                        ins = e.transpose(p16[:, h * 128:h * 128 + sz], ktm16[0:sz, b, h * 128:(h + 1) * 128],
                                          ident16[0:sz, 0:sz])
                    return ins
                S.op("pe", tfn, reads=[("ktm", b), "ident"], writes=[pk])
                src = p16.rearrange("p (h k) -> p h k", k=128)[:, :, 0:sz]
                if kind == "sample":
                    dstT = KTn[:, :, off:off + sz]
                    wkey = ("KTn", b)
                else:
                    pos = tl["pos0"] + off
                    dstT = KT[:, :, pos:pos + sz]
                    wkey = ("KT", tl["kb0"] + b)
                S.op("dve", lambda e, dstT=dstT, src=src: e.tensor_scalar(dstT, src, 1.0, None, ALU.mult), reads=[pk], writes=[wkey])

            def h_sv(b, off, sz, g, ps_ap, pk):
                s_, sk_ = strot.get()
                S.op("act", lambda e: e.activation(s_[0:sz, :], ps_ap, AF.Copy), reads=[pk], writes=[sk_])
                for dfull in tl["vout"][b]:
                    dst = dfull[:, g * G:(g + 1) * G]
                    S.op(STQ, lambda e, dst=dst: e.dma_start(out=dst, in_=s_[0:sz, :]), reads=[sk_], dma=True)
                if kind == "sample":
                    S.op("dve", lambda e: e.tensor_scalar(Vn[0:sz, b, g * G:(g + 1) * G], s_[0:sz, :], 1.0, None, ALU.mult), reads=[sk_],
                         writes=[("Vn", b)])
                else:
                    S.op("dve", lambda e: e.tensor_scalar(Vc[0:sz, tl["kb0"] + b, g * G:(g + 1) * G], s_[0:sz, :], 1.0, None, ALU.mult), reads=[sk_],
                         writes=[("Vc", tl["kb0"] + b)])
            proj_tm(l, "sv", blocks, h_sv)

            att_ctr = [0]

            def attention(lanes, ncols, kblocks, finish):
                ai = att_ctr[0]
                att_ctr[0] += 1
                ops_ap, opsk = PS[4 + ai % 2]
                S.op("pool", lambda e: e.memset(Ls16[:, 0:ncols], 0.0), writes=["Ls16"])
                nkb = len(kblocks)

                def do_prep(i):
                    if i < nkb and kblocks[i].get("prep") is not None:
                        kblocks[i]["prep"](kblocks[i])

                def stageA1(bi):
                    kb = kblocks[bi]
                    nk = kb["nk"]
                    t = bi % 3
                    zi = rot_lo.get()
                    z_ap, zk = PS[zi]

                    def zfn(e):
                        ins = None
                        for li, (q_ap, qk, c0, w) in enumerate(lanes):
                            ins = e.matmul(z_ap[0:nk, c0:c0 + w], kb["kt"](li)[0], q_ap, start=True, stop=True)
                        return ins
                    rk = []
                    for li, (q_ap, qk, c0, w) in enumerate(lanes):
                        rk += [kb["kt"](li)[1], qk]
                    S.op("pe", zfn, reads=rk, writes=[zk])
                    e_, ek = e32[t], ("e32", t)
                    S.op("act", lambda e: e.activation(e_[0:nk, 0:ncols], z_ap[0:nk, 0:ncols], AF.Exp),
                         reads=[zk], writes=[ek])

                def stageA2(bi):
                    kb = kblocks[bi]
                    nk = kb["nk"]
                    t = bi % 3
                    e_, ek = e32[t], ("e32", t)
                    s_, sk_ = sp16[t], ("sp16", t)
                    S.op("act", lambda e: e.activation(s_[0:nk, 0:ncols], e_[0:nk, 0:ncols], AF.Ln, bias=1.0, scale=1.0),
                         reads=[ek], writes=[sk_])
                    if kb["m16"] is not None:
                        m16, m32, mkeys = kb["m16"], kb["m32"], kb["mkeys"]
                        S.op("pool", lambda e: e.tensor_tensor(s_[0:nk, 0:ncols], s_[0:nk, 0:ncols], m16, ALU.mult),
                             reads=[sk_] + mkeys, writes=[sk_])
                        S.op("pool", lambda e: e.tensor_tensor(e_[0:nk, 0:ncols], e_[0:nk, 0:ncols], m32, ALU.mult),
                             reads=[ek] + mkeys, writes=[ek])

                def stageB1a(bi):
                    kb = kblocks[bi]
                    nk = kb["nk"]
                    t = bi % 3
                    u = bi % 2
                    s_, sk_ = sp16[t], ("sp16", t)
                    ti = rot_lo.get()
                    t_ap, tk = PS[ti]
                    pairs = [(negtri16[0:nk, 0:nk], s_[0:nk, 0:ncols])]
                    rds = ["negtri", sk_]
                    if bi > 0:
                        pairs.append((negones16[:, 0:nk], Ls16[:, 0:ncols]))
                        rds += ["negones", "Ls16"]
                    mm_group(t_ap[0:nk, 0:ncols], tk, pairs, reads=rds)
                    r_, rk_ = r32[u], ("r32", u)
                    S.op("act", lambda e: e.activation(r_[0:nk, 0:ncols], t_ap[0:nk, 0:ncols], AF.Exp),
                         reads=[tk], writes=[rk_])

                def stageB1b(bi):
                    kb = kblocks[bi]
                    nk = kb["nk"]
                    t = bi % 3
                    u = bi % 2
                    e_, ek = e32[t], ("e32", t)
                    s_, sk_ = sp16[t], ("sp16", t)
                    r_, rk_ = r32[u], ("r32", u)
                    w_, wk_ = w16[u], ("w16", u)
                    S.op("dve", lambda e: e.tensor_tensor(w_[0:nk, 0:ncols], e_[0:nk, 0:ncols], r_[0:nk, 0:ncols], ALU.mult),
                         reads=[ek, rk_], writes=[wk_])
                    if bi < nkb - 1:
                        S.op("pool", lambda e: e.tensor_tensor(Ls16[0:nk, 0:ncols], Ls16[0:nk, 0:ncols], s_[0:nk, 0:ncols], ALU.add),
                             reads=[sk_, "Ls16"], writes=["Ls16"])

                def stageC(bi):
                    kb = kblocks[bi]
                    nk = kb["nk"]
                    u = bi % 2
                    w_, wk_ = w16[u], ("w16", u)

                    def ofn(e):
                        ins = None
                        for li, (q_ap, qk, c0, w) in enumerate(lanes):
                            ins = e.matmul(ops_ap[:, c0:c0 + w], kb["v"](li)[0], w_[0:nk, c0:c0 + w],
                                           start=(bi == 0 and li == 0), stop=(bi == nkb - 1))
                        return ins
                    rk = [wk_] + [kb["v"](li)[1] for li in range(len(lanes))]
                    S.op("pe", ofn, reads=rk, writes=[opsk])

                LOOK = 4
                for i in range(min(LOOK, nkb)):
                    do_prep(i)
                for i in range(min(2, nkb)):
                    stageA1(i)
                    stageA2(i)
                for bi in range(nkb):
                    do_prep(bi + LOOK)
                    if bi + 2 < nkb:
                        stageA1(bi + 2)
                    stageB1a(bi)
                    if bi + 2 < nkb:
                        stageA2(bi + 2)
                    stageB1b(bi)
                    if bi >= 1:
                        stageC(bi - 1)
                stageC(nkb - 1)
                finish(ops_ap, opsk)

            if kind != "sample":
                kb0 = tl["kb0"]
                for h in range(8):
                    kbl = []
                    for b in reversed(range(nb)):
                        off, sz = blocks[b]
                        pos = tl["pos0"] + off
                        kbl.append(dict(
                            nk=sz,
                            kt=lambda li, pos=pos, sz=sz, b=b, h=h: (KT[:, h, pos:pos + sz], ("KT", kb0 + b)),
                            v=lambda li, b=b, sz=sz, h=h: (Vc[0:sz, kb0 + b, h * 128:(h + 1) * 128], ("Vc", kb0 + b)),
                            m16=mask16[0:sz, 384 - off:384 - off + n], m32=mask32[0:sz, 384 - off:384 - off + n],
                            mkeys=["mask16", "mask32"]))
                    for kb in reversed(range(kb0)):
                        if kb == 0:
                            pos, sz = 0, N_META
                        else:
                            pos, sz = N_META + (kb - 1) * 128, 128
                        kbl.append(dict(
                            nk=sz,
                            kt=lambda li, pos=pos, sz=sz, kb=kb, h=h: (KT[:, h, pos:pos + sz], ("KT", kb)),
                            v=lambda li, kb=kb, sz=sz, h=h: (Vc[0:sz, kb, h * 128:(h + 1) * 128], ("Vc", kb)),
                            m16=None, m32=None, mkeys=None))

                    def fin(ops_ap, opsk, h=h):
                        S.op("dve", lambda e: e.tensor_tensor(osbT[:, h, 0:n], ops_ap[:, 0:n], szs[:, h, 0:n], ALU.mult),
                             reads=[opsk, ("szs", h)], writes=[("osbT", h)])
                    attention([(sqT[:, h, 0:n], ("sqT", h), 0, n)], n, kbl, fin)
            else:
                for c, (off, sz) in enumerate(blocks):
                    sidx = tl["seqs"][c]
                    lanes = [(sqT[:, h, off:off + sz], ("sqT", h), h * 64, 64) for h in range(8)]
                    kbl = [dict(
                        nk=sz,
                        kt=lambda li, off=off, sz=sz, c=c: (KTn[:, li, off:off + sz], ("KTn", c)),
                        v=lambda li, c=c, sz=sz: (Vn[0:sz, c, li * 128:(li + 1) * 128], ("Vn", c)),
                        m16=masks16[0:sz, :, :].rearrange("p a b -> p (a b)"),
                        m32=masks32[0:sz, :, :].rearrange("p a b -> p (a b)"),
                        mkeys=["masks16", "masks32"])]
                    for pb in reversed(range(NPB)):
                        def prep(kb, pb=pb, sidx=sidx):
                            k_, kk = kstrot.get()
                            v_, vk = vstrot.get()
                            kb_, kbk = k16rot.get()
                            vb_, vbk = v16rot.get()
                            kt_, ktk = ktsrot.get()
                            S.op("sp", lambda e: e.dma_start(out=k_, in_=ck[l, sidx, pb * 128:(pb + 1) * 128, :]),
                                 writes=[kk], dma=True)
                            S.op("sp", lambda e: e.dma_start(out=v_, in_=cv[l, sidx, pb * 128:(pb + 1) * 128, :]),
                                 writes=[vk], dma=True)
                            S.op("dve", lambda e: e.tensor_scalar(kb_, k_, 1.0, None, ALU.mult), reads=[kk], writes=[kbk])
                            S.op("pool", lambda e: e.tensor_copy(vb_, v_), reads=[vk], writes=[vbk])
                            pi = 6 + (pb % 2)
                            p16, pk = PS16[pi]

                            def tfn(e):
                                ins = None
                                for h in range(8):
                                    ins = e.transpose(p16[:, h * 128:(h + 1) * 128], kb_[:, h * 128:(h + 1) * 128], ident16)
                                return ins
                            S.op("pe", tfn, reads=[kbk, "ident"], writes=[pk])
                            S.op("dve", lambda e: e.tensor_scalar(kt_.rearrange("p a b -> p (a b)"), p16, 1.0, None, ALU.mult),
                                 reads=[pk], writes=[ktk])
                            kb["kt"] = lambda li: (kt_[:, li, :], ktk)
                            kb["v"] = lambda li: (vb_[:, li * 128:(li + 1) * 128], vbk)
                        kbl.append(dict(nk=128, prep=prep, m16=None, m32=None, mkeys=None))

                    def fin(ops_ap, opsk, off=off, sz=sz):
                        for h in range(8):
                            S.op("dve", lambda e, h=h: e.tensor_tensor(osbT[:, h, off:off + sz], ops_ap[:, h * 64:h * 64 + sz],
                                                                       szs[:, h, off:off + sz], ALU.mult),
                                 reads=[opsk, ("szs", h)], writes=[("osbT", h)])
                    attention(lanes, 512, kbl, fin)

            phase_reset()
            mrg = A.alloc((KC, 512), BF16)
            top_mrg = A.top
            sgt = [A.alloc((512,), F32) for _ in range(4)]
            for jp in range(8):
                wg, wgk = load_w(l, "in", SEC["mg"][0] + jp * G, G, KC)
                ws, wsk = load_w(l, "in", SEC["ms"][0] + jp * G, G, KC)
                wug, wugk = load_w(l, "ug", jp * G, G, 8)
                wus, wusk = load_w(l, "us", jp * G, G, 8)
                for jj in range(2):
                    j = jp * 2 + jj
                    o0 = jj * 128
                    base = (j % 2) * 4
                    (p_mg, k_mg), (p_ms, k_ms), (p_yg, k_yg), (p_ys, k_ys) = [PS[base + q] for q in range(4)]
                    mm_group(p_mg[:, 0:n], k_mg, [(wg[:, kc, o0:o0 + 128], hT[:, kc, 0:n]) for kc in range(KC)],
                             reads=[wgk, "hT"])
                    mm_group(p_ms[:, 0:n], k_ms, [(ws[:, kc, o0:o0 + 128], hT[:, kc, 0:n]) for kc in range(KC)],
                             reads=[wsk, "hT"])
                    mm_group(p_yg[:, 0:n], k_yg, [(wug[:, kc, o0:o0 + 128], ogT[:, kc, 0:n]) for kc in range(8)],
                             reads=[wugk] + [("ogT", q) for q in range(8)])
                    mm_group(p_ys[:, 0:n], k_ys, [(wus[:, kc, o0:o0 + 128], osbT[:, kc, 0:n]) for kc in range(8)],
                             reads=[wusk] + [("osbT", q) for q in range(8)])
                    g1, g1k = sgt[(j % 2) * 2], ("sgt", (j % 2) * 2)
                    g2, g2k = sgt[(j % 2) * 2 + 1], ("sgt", (j % 2) * 2 + 1)
                    S.op("act", lambda e, g1=g1, p_mg=p_mg: e.activation(g1[:, 0:n], p_mg[:, 0:n], AF.Sigmoid),
                         reads=[k_mg], writes=[g1k])
                    S.op("act", lambda e, g2=g2, p_ms=p_ms: e.activation(g2[:, 0:n], p_ms[:, 0:n], AF.Sigmoid),
                         reads=[k_ms], writes=[g2k])
                    S.op("dve", lambda e, g1=g1, p_yg=p_yg: e.tensor_tensor(g1[:, 0:n], g1[:, 0:n], p_yg[:, 0:n], ALU.mult),
                         reads=[g1k, k_yg], writes=[g1k])
                    S.op("dve", lambda e, g2=g2, p_ys=p_ys: e.tensor_tensor(g2[:, 0:n], g2[:, 0:n], p_ys[:, 0:n], ALU.mult),
                         reads=[g2k, k_ys], writes=[g2k])
                    S.op("pool", lambda e, g1=g1, g2=g2, j=j: e.tensor_tensor(mrg[:, j, 0:n], g1[:, 0:n], g2[:, 0:n], ALU.add),
                         reads=[g1k, g2k], writes=["mrg"])

            S.barrier(lambda e: e.memset(dummy[:, 0:1], 0.0))
            A.top = top_mrg
            u32 = u32_alias
            xin2 = [A.alloc((D,), F32) for _ in range(2)]
            junk = A.alloc((D,), BF16)
            for g in range(D // G):
                wo, wok = load_w(l, "o", g * G, G, KC)
                for b, (off, sz) in enumerate(blocks):
                    pi = rot_all.get()
                    ps_ap, pk = PS[pi]
                    mm_group(ps_ap[0:sz, 0:G], pk, [(mrg[:, kc, off:off + sz], wo[:, kc, 0:G]) for kc in range(KC)],
                             reads=[wok, "mrg"])
                    S.op("act", lambda e, b=b, sz=sz, g=g, ps_ap=ps_ap: e.activation(u32[0:sz, b, g * G:(g + 1) * G],
                                                                                     ps_ap[0:sz, 0:G], AF.Copy),
                         reads=[pk], writes=[("u32", b)])
            for b, (off, sz) in enumerate(blocks):
                sm, smk = small_rot.get()
                xi, xk = xin2[b % 2], ("xin2", b % 2)
                src = tl["xsrc"][b]
                S.op("sp", lambda e, xi=xi, sz=sz, src=src: e.dma_start(out=xi[0:sz, :], in_=src),
                     reads=tl["xsrc_keys"][b], writes=[xk], dma=True)
                S.op("act", lambda e, b=b, sz=sz, sm=sm: e.activation(junk[0:sz, :], u32[0:sz, b, :], AF.Square,
                                                                      accum_out=sm[0:sz, 0:1]),
                     reads=[("u32", b)], writes=["junk", smk])
                rstd_from_ssq(sm[0:sz, 0:1], smk, sz, 1.0 / D)
                S.op("dve", lambda e, b=b, sz=sz, sm=sm: e.scalar_tensor_tensor(u32[0:sz, b, :], u32[0:sz, b, :], sm[0:sz, 0:1],
                                                                                pgb[0:sz, :], ALU.mult, ALU.mult),
                     reads=[("u32", b), smk, "pgb"], writes=[("u32", b)])
                S.op("pool", lambda e, b=b, sz=sz, xi=xi: e.tensor_tensor(xi[0:sz, :], xi[0:sz, :], u32[0:sz, b, :], ALU.add),
                     reads=[("u32", b), xk], writes=[xk])
                for (dst, dkey) in tl["ysink"][b]:
                    S.op(STQ, lambda e, dst=dst, xi=xi, sz=sz: e.dma_start(out=dst, in_=xi[0:sz, :]),
                         reads=[xk], writes=dkey, dma=True)

        convert_layer(0)
        conv_some(10 ** 6)
        for l in (range(DEPTH) if cfg.get('stop') != 0 else []):
            S.barrier(lambda e: e.memset(dummy[:, 0:1], 0.0))
            if cfg.get('stop') is not None and l > 0:
                break
            S.op("sp", lambda e, l=l: e.dma_start(out=pg_fm, in_=pre_g[l]), writes=["pg_fm"], dma=True)
            S.op("sp", lambda e, l=l: e.dma_start(out=pgb, in_=post_g[l]), writes=["pgb"], dma=True)
            S.op("sp", lambda e, l=l: e.dma_start(out=gg, in_=gla_g[l]), writes=["gg"], dma=True)
            S.op("sp", lambda e, l=l: e.dma_start(out=negba, in_=b_a[l]), writes=["negba"], dma=True)
            S.op("act", lambda e: e.activation(negba, negba, AF.Copy, scale=-1.0), reads=["negba"], writes=["negba"])
            S.op("pool", lambda e, l=l: e.dma_start(out=wa2_16[0:16, :], in_=w_a2[l]), writes=["wa2"], dma=True)
            if l + 1 < DEPTH:
                convert_layer(l + 1)
            conv_per_tile = -(-len(conv_q) // (NP * NT))
            for b in range(NP):
                tiles = [(0, N_META)] + [(N_META + 512 * m, 512) for m in range(NT)]
                for ti, (t0, n) in enumerate(tiles):
                    if ti == 0 and b > 0:
                        continue
                    blocks = [(o, min(128, n - o)) for o in range(0, n, 128)]
                    xsrc, xkeys, ysink, kout, vout = [], [], [], [], []
                    for (off, sz) in blocks:
                        r0 = t0 + off
                        if l == 0:
                            if ti == 0:
                                xsrc.append(meta[0:sz, :])
                            else:
                                xsrc.append(xp[b, r0 - N_META:r0 - N_META + sz, :])
                            xkeys.append([])
                        else:
                            xsrc.append(yscr_p[b, r0:r0 + sz, :])
                            xkeys.append([("yscr_p", b, r0)])
                        sinks = []
                        if l < DEPTH - 1:
                            sinks.append((yscr_p[b, r0:r0 + sz, :], [("yscr_p", b, r0)]))
                        elif ti > 0:
                            sinks.append((yp[b, r0 - N_META:r0 - N_META + sz, :], []))
                        ysink.append(sinks)
                        if ti == 0:
                            kout.append([kp[l, bb, r0:r0 + sz, :] for bb in range(NP)])
                            vout.append([vp[l, bb, r0:r0 + sz, :] for bb in range(NP)])
                        else:
                            kout.append([kp[l, b, r0:r0 + sz, :]])
                            vout.append([vp[l, b, r0:r0 + sz, :]])
                    tl = dict(kind="prompt", n=n, blocks=blocks, xsrc=xsrc, xsrc_keys=xkeys, ysink=ysink,
                              kout=kout, vout=vout, pos0=t0, kb0=(0 if ti == 0 else 1 + 4 * (ti - 1)),
                              first=(ti == 0), final=(ti == len(tiles) - 1), seq=b,
                              save_meta=(ti == 0 and NP > 1), init_from_meta=(ti == 1 and b > 0))
                    run_tile(l, tl)
                    conv_some(conv_per_tile)
            conv_some(10 ** 6)
            for s0 in range(0, NS, 4):
                seqs = list(range(s0, min(NS, s0 + 4)))
                blocks = [(64 * i, 64) for i in range(len(seqs))]
                xsrc, xkeys, ysink, kout, vout = [], [], [], [], []
                for s in seqs:
                    if l == 0:
                        xsrc.append(xs[s])
                        xkeys.append([])
                    else:
                        xsrc.append(yscr_s[s])
                        xkeys.append([("yscr_s", s)])
                    if l < DEPTH - 1:
                        ysink.append([(yscr_s[s], [("yscr_s", s)])])
                    else:
                        ysink.append([(ys[s], [])])
                    kout.append([ksn[l, s]])
                    vout.append([vsn[l, s]])
                tl = dict(kind="sample", n=64 * len(seqs), blocks=blocks, xsrc=xsrc, xsrc_keys=xkeys, ysink=ysink,
                          kout=kout, vout=vout, seqs=seqs)
                run_tile(l, tl)
        S.emit(nc, st)
    return nc


def make_consts():
    p = np.arange(128)[:, None]
    g = np.arange(896)[None, :]
    mask = ((g - p - 384) > 0).astype(np.float32)
    j = np.arange(128)[:, None]
    s = np.arange(128)[None, :]
    negtri = -(j >= s).astype(np.float32)
    rst = np.ones((2, 128, 512), np.float32)
    rst[0, :, ::128] = 0.0
    rst[1, :, ::64] = 0.0
    return dict(c_ident=np.eye(128, dtype=np.float32), c_negtri=negtri, c_mask=mask, c_rst=rst)


def run_cfg(cfg, n_cores, inputs):
    NP, NS, DEPTH = cfg["NP"], cfg["NS"], cfg["DEPTH"]
    f = lambda a: np.ascontiguousarray(np.asarray(a, dtype=np.float32))
    nc = build_program(cfg)
    consts = make_consts()
    shared = dict(
        meta=f(inputs["meta_tokens"]),
        pre_g=f(np.asarray(inputs["pre_gain"]).reshape(DEPTH, 16, 128).transpose(0, 2, 1)),
        post_g=f(np.broadcast_to(np.asarray(inputs["post_gain"])[:, None, :], (DEPTH, 128, D))),
        w_in=f(inputs["w_in"]), w_a2=f(inputs["w_a2"]),
        b_a=f(np.asarray(inputs["b_a"]).reshape(DEPTH, 4, 128).transpose(0, 2, 1)),
        gla_g=f(np.asarray(inputs["gla_gain"]).reshape(DEPTH, 8, 128).transpose(0, 2, 1)),
        w_ug=f(inputs["w_up_gla"]), w_us=f(inputs["w_up_sb"]), w_o=f(inputs["w_o"]), **consts)
    in_maps = []
    for c in range(n_cores):
        m = dict(shared)
        m["xp"] = f(inputs["x_prompt"][c * NP:(c + 1) * NP])
        m["xs"] = f(inputs["x_sample"][c * NS:(c + 1) * NS])
        m["ck"] = f(np.asarray(inputs["cache_sb_k"])[:, c * NS:(c + 1) * NS].reshape(DEPTH, NS, cfg["PAST"], 1024))
        m["cv"] = f(np.asarray(inputs["cache_sb_v"])[:, c * NS:(c + 1) * NS].reshape(DEPTH, NS, cfg["PAST"], 1024))
        m["sg"] = f(np.asarray(inputs["state_gla"])[:, c * NS:(c + 1) * NS])
        in_maps.append(m)
    res = run_bass_kernel_spmd(nc, in_maps, core_ids=list(range(n_cores)))
    R = res.results
    if cfg.get('dbg'):
        return R
    T = N_META + cfg["SEQ"]
    cat = lambda k, ax: np.concatenate([np.asarray(r[k]) for r in R], axis=ax)
    y_prompt = cat("yp", 0)
    y_sample = cat("ys", 0)
    k_p = cat("kp", 1).reshape(DEPTH, n_cores * NP, T, 8, 128)
    v_p = cat("vp", 1).reshape(DEPTH, n_cores * NP, T, 8, 128)
    g_p = cat("gp", 1)
    k_s = cat("ksn", 1).reshape(DEPTH, n_cores * NS, 64, 8, 128)
    v_s = cat("vsn", 1).reshape(DEPTH, n_cores * NS, 64, 8, 128)
    g_s = cat("gs", 1)
    return (y_prompt, y_sample, k_p, v_p, g_p, k_s, v_s, g_s)


def kernel(x_prompt, x_sample, cache_sb_k, cache_sb_v, state_gla, meta_tokens, pre_gain, w_in, w_a2,
           b_a, gla_gain, w_up_gla, w_up_sb, w_o, post_gain):
    cfg = dict(NP=2, SEQ=2048, NS=4, PAST=4096, DEPTH=2)
    inputs = dict(x_prompt=np.asarray(x_prompt), x_sample=np.asarray(x_sample), cache_sb_k=cache_sb_k,
                  cache_sb_v=cache_sb_v, state_gla=state_gla, meta_tokens=meta_tokens, pre_gain=pre_gain,
                  w_in=w_in, w_a2=w_a2, b_a=b_a, gla_gain=gla_gain, w_up_gla=w_up_gla, w_up_sb=w_up_sb,
                  w_o=w_o, post_gain=post_gain)
    return run_cfg(cfg, 8, inputs)
```
